# Optimizing a Trainium2 kernel written in Bass

```python
import jax, jax.numpy as jnp
from jax import lax
import numpy as np

D_MODEL = 1024
BATCH = 4
SEQ = 8192
DEPTH = 1

CHUNK = 64
N_MEM = 256
NORM_EPS = 1e-6

RW_HEAD = 64
RW_HEADS = D_MODEL // RW_HEAD
RW_WIDTH = RW_HEADS * RW_HEAD
RW_DECAY_RANK = 64
RW_ICLR_RANK = 64
RW_GATE_RANK = 160
RW_LN_EPS = 64e-5
RW_COLS = 3 * RW_WIDTH + RW_DECAY_RANK + RW_ICLR_RANK + RW_GATE_RANK
RW_SPLITS = (RW_WIDTH, 2 * RW_WIDTH, 3 * RW_WIDTH,
             3 * RW_WIDTH + RW_DECAY_RANK,
             3 * RW_WIDTH + RW_DECAY_RANK + RW_ICLR_RANK)

HG_KEY = 128
HG_HEADS = D_MODEL // HG_KEY
HG_VAL = D_MODEL // HG_HEADS
HG_WIDTH = HG_HEADS * HG_KEY
HG_VWIDTH = HG_HEADS * HG_VAL
HG_COLS = 2 * HG_WIDTH + 2 * HG_VWIDTH
HG_SPLITS = (HG_WIDTH, 2 * HG_WIDTH, 2 * HG_WIDTH + HG_VWIDTH)

GATE_COLS = 2 * D_MODEL
IN_COLS = RW_COLS + HG_COLS + GATE_COLS

XA_HEADS = 4
XA_HEAD = D_MODEL // XA_HEADS

D_FF = ((8 * D_MODEL + 3 * 256 - 1) // (3 * 256)) * 256

kernel_name = 'rwkv7_hgrn2_griffin_merge_block'


def rmsnorm(x, g, eps=NORM_EPS):
    xf = x.astype(jnp.float32)
    y = xf * lax.rsqrt(jnp.mean(xf * xf, axis=-1, keepdims=True) + eps)
    return (y * g.astype(jnp.float32)).astype(x.dtype)


def token_shift(u):
    return jnp.pad(u[:, :-1], ((0, 0), (1, 0), (0, 0)))


def rwkv7_branch(cols, mu, w0, w2, a0, a2, g2, k_k, k_a, r_k, ln_w, ln_b):
    B, S, _ = cols.shape
    f32 = jnp.float32
    cols = cols + mu * (token_shift(cols) - cols)
    r, k, v, w_lo, a_lo, g_lo = jnp.split(cols, RW_SPLITS, axis=-1)
    w_log = -jax.nn.softplus(-(w0 + jnp.tanh(w_lo) @ w2)) - 0.5
    decay = jnp.exp(-jnp.exp(w_log.astype(f32)))
    a = jax.nn.sigmoid(a0 + a_lo @ a2)
    g = jax.nn.sigmoid(g_lo) @ g2

    def heads(t):
        return t.astype(f32).reshape(B, S, RW_HEADS, RW_HEAD)

    kk = heads(k * k_k)
    kk = kk * lax.rsqrt(jnp.maximum(jnp.sum(kk * kk, axis=-1, keepdims=True), 1e-24))
    k_mod = heads(k * (1 + (a - 1) * k_a))
    r_h, v_h, a_h, w_h = heads(r), heads(v), heads(a), heads(decay)

    def step(state, inp):
        r_t, w_t, k_t, v_t, a_t, b_t = inp
        sa = jnp.einsum('bhvk,bhk->bhv', state, a_t)
        state = (state * w_t[:, :, None, :] + sa[..., None] * b_t[:, :, None, :]
                 + v_t[..., None] * k_t[:, :, None, :])
        return state, jnp.einsum('bhvk,bhk->bhv', state, r_t)

    tm = lambda t: jnp.swapaxes(t, 0, 1)
    s0 = jnp.zeros((B, RW_HEADS, RW_HEAD, RW_HEAD), f32)
    _, o = lax.scan(step, s0, (tm(r_h), tm(w_h), tm(k_mod), tm(v_h), tm(-kk), tm(kk * a_h)))
    o = jnp.swapaxes(o, 0, 1)
    mean = jnp.mean(o, axis=-1, keepdims=True)
    var = jnp.mean(jnp.square(o - mean), axis=-1, keepdims=True)
    o = ((o - mean) * lax.rsqrt(var + RW_LN_EPS)).reshape(B, S, RW_WIDTH) * ln_w + ln_b
    bonus = jnp.sum(r_h * k_mod * r_k, axis=-1, keepdims=True) * v_h
    o = o + bonus.reshape(B, S, RW_WIDTH)
    return (o * g).astype(cols.dtype)


def hgrn2_branch(cols, lb, norm_g):
    B, S, _ = cols.shape
    f32 = jnp.float32
    q, f_raw, i, g = jnp.split(cols, HG_SPLITS, axis=-1)
    f = lb + (1 - lb) * jax.nn.sigmoid(f_raw.astype(f32))
    log_f = jnp.log(f)
    k = 1 - f
    q = jax.nn.silu(q.astype(f32))
    n_chunks = S // CHUNK

    def chunked(t, d):
        return t.astype(f32).reshape(B, n_chunks, CHUNK, HG_HEADS, d).transpose(1, 0, 3, 2, 4)

    causal = jnp.tril(jnp.ones((CHUNK, CHUNK), dtype=bool))[:, :, None]

    def chunk_step(state, inp):
        q_c, k_c, i_c, lf_c = inp
        b = jnp.cumsum(lf_c, axis=2)
        rel = jnp.exp(jnp.where(causal, b[:, :, :, None, :] - b[:, :, None, :, :], -jnp.inf))
        scores = jnp.einsum('bhtk,bhtsk,bhsk->bhts', q_c, rel, k_c)
        o = (jnp.einsum('bhts,bhsv->bhtv', scores, i_c)
             + jnp.einsum('bhtk,bhkv->bhtv', q_c * jnp.exp(b), state))
        b_end = b[:, :, -1:, :]
        state = (jnp.exp(b_end[:, :, 0, :])[..., None] * state
                 + jnp.einsum('bhsk,bhsv->bhkv', k_c * jnp.exp(b_end - b), i_c))
        return state, o

    s0 = jnp.zeros((B, HG_HEADS, HG_KEY, HG_VAL), f32)
    _, o = lax.scan(chunk_step, s0, (chunked(q, HG_KEY), chunked(k, HG_KEY),
                                     chunked(i, HG_VAL), chunked(log_f, HG_KEY)))
    o = o.transpose(1, 0, 3, 2, 4).reshape(B, S, HG_HEADS, HG_VAL)
    o = o * lax.rsqrt(jnp.mean(o * o, axis=-1, keepdims=True) + NORM_EPS) * norm_g
    o = o.reshape(B, S, HG_VWIDTH) * jax.nn.silu(g.astype(f32))
    return o.astype(cols.dtype)


def setup_inputs(seed: int = 0) -> dict:
    key = jax.random.key(seed)
    ks = iter(jax.random.split(key, 32))
    L, D = DEPTH, D_MODEL

    def nrm(shape, scale):
        return scale * jax.random.normal(next(ks), shape, jnp.float32)

    def gain(shape):
        return 1.0 + nrm(shape, 0.02)

    return {
        'x': nrm((BATCH, SEQ, D), 1.0),
        'mem': nrm((BATCH, N_MEM, D), 1.0),
        'norm_mix_g': gain((L, D)),
        'w_in': nrm((L, D, IN_COLS), D ** -0.5),
        'rw_mu': jax.random.uniform(next(ks), (L, RW_COLS), jnp.float32),
        'rw_w0': -1.0 + nrm((L, RW_WIDTH), 0.5),
        'rw_w2': nrm((L, RW_DECAY_RANK, RW_WIDTH), 0.5 * RW_DECAY_RANK ** -0.5),
        'rw_a0': nrm((L, RW_WIDTH), 0.1),
        'rw_a2': nrm((L, RW_ICLR_RANK, RW_WIDTH), 0.5 * RW_ICLR_RANK ** -0.5),
        'rw_g2': nrm((L, RW_GATE_RANK, RW_WIDTH), RW_GATE_RANK ** -0.5),
        'rw_k_k': 0.85 + nrm((L, RW_WIDTH), 0.02),
        'rw_k_a': gain((L, RW_WIDTH)),
        'rw_r_k': nrm((L, RW_HEADS, RW_HEAD), 0.1),
        'rw_ln_w': gain((L, RW_WIDTH)),
        'rw_ln_b': nrm((L, RW_WIDTH), 0.02),
        'hg_lb_logits': nrm((L + 1, HG_WIDTH), 0.5),
        'hg_norm_g': gain((L, HG_VAL)),
        'w_out': nrm((L, D, D), D ** -0.5),
        'norm_xa_g': gain((L, D)),
        'norm_mem_g': gain((L, D)),
        'xa_wq': nrm((L, D, D), D ** -0.5),
        'xa_wk': nrm((L, D, D), D ** -0.5),
        'xa_wv': nrm((L, D, D), D ** -0.5),
        'xa_wo': nrm((L, D, D), D ** -0.5),
        'norm_ffn_g': gain((L, D)),
        'ffn_w1': nrm((L, D, D_FF), D ** -0.5),
        'ffn_w3': nrm((L, D, D_FF), D ** -0.5),
        'ffn_w2': nrm((L, D_FF, D), D_FF ** -0.5),
        'norm_final_g': gain((D,)),
    }


def reference(x, mem, norm_mix_g, w_in, rw_mu, rw_w0, rw_w2, rw_a0, rw_a2, rw_g2,
              rw_k_k, rw_k_a, rw_r_k, rw_ln_w, rw_ln_b, hg_lb_logits, hg_norm_g, w_out,
              norm_xa_g, norm_mem_g, xa_wq, xa_wk, xa_wv, xa_wo,
              norm_ffn_g, ffn_w1, ffn_w3, ffn_w2, norm_final_g):
    B, S, _ = x.shape
    n_mem = mem.shape[1]
    lb_all = jnp.cumsum(jax.nn.softmax(hg_lb_logits.astype(jnp.float32), axis=0), axis=0)
    h = x
    for l in range(DEPTH):
        u = rmsnorm(h, norm_mix_g[l])
        cols = u @ w_in[l]
        rw_cols, hg_cols, gate_cols = jnp.split(cols, (RW_COLS, RW_COLS + HG_COLS), axis=-1)
        y_a = rwkv7_branch(rw_cols, rw_mu[l], rw_w0[l], rw_w2[l], rw_a0[l], rw_a2[l], rw_g2[l],
                           rw_k_k[l], rw_k_a[l], rw_r_k[l], rw_ln_w[l], rw_ln_b[l])
        y_b = hgrn2_branch(hg_cols, lb_all[l], hg_norm_g[l])
        gate_a, gate_b = jnp.split(jax.nn.sigmoid(gate_cols), 2, axis=-1)
        h = h + (gate_a * y_a + gate_b * y_b) @ w_out[l]

        u = rmsnorm(h, norm_xa_g[l])
        m = rmsnorm(mem, norm_mem_g[l])
        q = (u @ xa_wq[l]).reshape(B, S, XA_HEADS, XA_HEAD)
        k = (m @ xa_wk[l]).reshape(B, n_mem, XA_HEADS, XA_HEAD)
        v = (m @ xa_wv[l]).reshape(B, n_mem, XA_HEADS, XA_HEAD)
        s = jnp.einsum('bqhd,bmhd->bhqm', q, k).astype(jnp.float32) * (XA_HEAD ** -0.5)
        p = jax.nn.softmax(s, axis=-1).astype(v.dtype)
        o = jnp.einsum('bhqm,bmhd->bqhd', p, v).reshape(B, S, D_MODEL)
        h = h + o @ xa_wo[l]

        u = rmsnorm(h, norm_ffn_g[l])
        h = h + (jax.nn.silu(u @ ffn_w1[l]) * (u @ ffn_w3[l])) @ ffn_w2[l]
    return rmsnorm(h, norm_final_g)
```

```python
import numpy as np
from collections import deque
from contextlib import ExitStack
import concourse.bass as bass
import concourse.mybir as mybir
from concourse.bass_utils import run_bass_kernel_spmd

F32 = mybir.dt.float32
ALU = mybir.AluOpType
AF = mybir.ActivationFunctionType
ENGS = ("pe", "act", "dve", "pool", "sp")

D = 1024
T = 512
CDEC = float(np.exp(-0.5))
NORM_EPS = 1e-6
RW_LN_EPS = 64e-5

IN_M = [128] * 14 + [32] + [128] * 24


def own_chunks(half):
    hs = [4 * half + j for j in range(4)]
    ch = [(h * 128, 128) for h in hs] + [(1024 + h * 128, 128) for h in hs] + [(2048 + h * 128, 128) for h in hs]
    ch += [(3072, 128), (3200, 128), (3328, 32)]
    for base in (3360, 4384, 5408, 6432, 7456, 8480):
        ch += [(base + h * 128, 128) for h in hs]
    return ch
C_ID, C_ONES, C_BONES, C_IDP, C_HGM, C_RWM2, C_RWM, C_SCM, NCONST = 0, 128, 256, 384, 448, 576, 704, 1216, 1728
PCOL = {}
_o = 0
for _n, _w in [("g_mix", 8), ("mu", 15), ("w0", 4), ("a0", 4), ("k_k", 4), ("k_a", 4), ("r_k", 4), ("ln_w", 4),
               ("ln_b", 4), ("l0", 4), ("l1", 4), ("hg_g", 1), ("g_xa", 8), ("g_mem", 8), ("g_ffn", 8), ("g_fin", 8)]:
    PCOL[_n] = _o
    _o += _w
NPAR = _o


class Buf:
    __slots__ = ("name", "writers", "readers", "dma_sem", "dma_cnt")

    def __init__(self, name=""):
        self.name = name
        self.writers = []
        self.readers = []
        self.dma_sem = None
        self.dma_cnt = 0


class Op:
    __slots__ = ("eng", "fn", "idx", "waits", "signal", "semval", "kind", "dsem", "dval", "cc")

    def __init__(self, eng, fn, idx, kind):
        self.eng, self.fn, self.idx, self.kind = eng, fn, idx, kind
        self.waits = []
        self.signal = False
        self.semval = 0
        self.dsem = None
        self.dval = 0
        self.cc = False


class Sched:
    def __init__(self, nc):
        self.nc = nc
        self.ops = {e: [] for e in ENGS}
        self.waited = {e: {} for e in ENGS}
        self.dma_bufs = []

    def _dep(self, op, d):
        if d is op:
            return
        w = self.waited[op.eng]
        if d.kind == "d":
            key = ("d", id(d.dsem))
            if w.get(key, -1) >= d.dval:
                return
            w[key] = d.dval
            op.waits.append(d)
            return
        if d.eng == op.eng and op.kind == "c":
            if op.eng == "pe":
                return
        key = ("e", d.eng)
        if w.get(key, -1) >= d.idx:
            return
        w[key] = d.idx
        d.signal = True
        op.waits.append(d)

    def _mk(self, eng, fn, reads, writes, kind):
        op = Op(eng, fn, len(self.ops[eng]), kind)
        for b in reads:
            for d in b.writers:
                self._dep(op, d)
        for b in writes:
            for d in b.writers:
                self._dep(op, d)
            for d in b.readers:
                self._dep(op, d)
        for b in reads:
            b.readers.append(op)
        for b in writes:
            b.writers = [op]
            b.readers = []
        self.ops[eng].append(op)
        return op

    def add(self, eng, fn, reads=(), writes=()):
        return self._mk(eng, fn, reads, writes, "c")

    def dma(self, eng, fn, reads=(), writes=(), sbuf=None):
        op = self._mk(eng, fn, reads, writes, "d")
        if sbuf.dma_sem is None:
            sbuf.dma_sem = "pending"
            self.dma_bufs.append(sbuf)
        sbuf.dma_cnt += 16
        op.dsem = sbuf
        op.dval = sbuf.dma_cnt
        return op

    def coll(self, fn, reads, writes, sbuf):
        op = self._mk("pool", fn, reads, writes, "d")
        sbuf.dma_sem = "pending"
        self.dma_bufs.append(sbuf)
        sbuf.dma_cnt = 1
        op.dsem = sbuf
        op.dval = 1
        op.cc = True
        return op

    def emit(self):
        nc = self.nc
        with ExitStack() as st:
            esem = {e: st.enter_context(nc.semaphore("sem_" + e)) for e in ENGS}
            for i, b in enumerate(self.dma_bufs):
                b.dma_sem = st.enter_context(nc.semaphore("dsem%d" % i))
            for e in ENGS:
                c = 0
                for op in self.ops[e]:
                    if op.kind == "c" and op.signal:
                        c += 1
                        op.semval = c
            block = st.enter_context(nc.Block())

            def run(e, h):
                for op in self.ops[e]:
                    for d in op.waits:
                        if d.kind == "d":
                            h.wait_ge(d.dsem.dma_sem, d.dval)
                        else:
                            h.wait_ge(esem[d.eng], d.semval)
                    ins = op.fn(h)
                    if op.cc:
                        ins.then_inc(op.dsem.dma_sem)
                    elif op.kind == "d":
                        ins.then_inc(op.dsem.dma_sem, 16)
                    elif op.signal:
                        ins.then_inc(esem[e], 1)
                if e == "sp":
                    for b in self.dma_bufs:
                        h.wait_ge(b.dma_sem, b.dma_cnt)

            @block.tensor
            def _(h):
                run("pe", h)

            @block.scalar
            def _(h):
                run("act", h)

            @block.vector
            def _(h):
                run("dve", h)

            @block.gpsimd
            def _(h):
                run("pool", h)

            @block.sync
            def _(h):
                run("sp", h)


class Slab:
    __slots__ = ("t", "b")

    def __init__(self, t, b):
        self.t = t
        self.b = b


class _Stop(Exception):
    pass


def build_program(n_mix, n_tok, n_cores=8, debug=None, stop_after=None):
    NT = n_mix
    n_pre, n_full = 0, n_mix
    nc = bass.Bass("TRN2", target_bir_lowering=False)

    def din(name, shape):
        return nc.dram_tensor(name, shape, F32, kind="ExternalInput").ap()

    xT_d = din("xT", [D, NT * T])
    xtok_d = din("xtok", [D, n_tok * T])
    sel_d = din("sel", [128, 2])
    memT_d = din("memT", [D, 256])
    consts_d = din("consts", [128, NCONST])
    params_d = din("params", [128, NPAR])
    w_in_d = din("w_in", [39, 128, 1024])
    lw_d = din("lw", [128, 512])
    g2a_d = din("g2a", [128, 512])
    g2b_d = din("g2b", [32, 512])
    yloc_t = [nc.dram_tensor("ymix_loc%d" % i, [512, T], F32) for i in range(NT)]
    yall_t = [nc.dram_tensor("ymix_all%d" % i, [1024, T], F32) for i in range(NT)]
    w_out_d = din("w_out", [8, 128, 1024])
    wq_d = din("wq", [8, 128, 1024])
    wk_d = din("wk", [8, 128, 1024])
    wv_d = din("wv", [8, 128, 1024])
    wo_d = din("wo", [8, 128, 1024])
    w1_d = din("w1", [22, 128, 1024])
    w3_d = din("w3", [22, 128, 1024])
    w2_d = din("w2", [24, 128, 1024])
    outT_d = nc.dram_tensor("outT", [D, n_tok * T], F32, kind="ExternalOutput").ap()
    dbg_out = {}

    s = Sched(nc)
    with ExitStack() as st:
        def sbt(name, shape):
            return st.enter_context(nc.sbuf_tensor(name, shape, F32))

        def pst(name):
            return st.enter_context(nc.psum_tensor(name, [128, 512], F32))

        CON = sbt("CON", [128, NCONST]); bCON = Buf("CON")
        PAR = sbt("PAR", [128, NPAR]); bPAR = Buf("PAR")
        PD = sbt("PD", [128, 16]); bPD = Buf("PD")
        SEL = sbt("SEL", [128, 2]); bSEL = Buf("SEL")
        LW = sbt("LW", [128, 512]); bLW = Buf("LW")
        G2A = sbt("G2A", [128, 512]); bG2A = Buf("G2A")
        G2B = sbt("G2B", [32, 512]); bG2B = Buf("G2B")
        bYL = [Buf("YL%d" % i) for i in range(NT)]
        bYALL = [Buf("YALL%d" % i) for i in range(NT)]; bCC = [Buf("CC%d" % i) for i in range(NT)]
        KTM = sbt("KTM", [128, 8 * 256]); bKTM = Buf("KTM")
        VM = sbt("VM", [128, 2 * 1024]); bVM = Buf("VM")
        SRW = sbt("SRW", [128, 4 * 64]); bSRW = [Buf("SRW%d" % i) for i in range(4)]
        SHG = sbt("SHG", [128, 4 * 128]); bSHG = [Buf("SHG%d" % i) for i in range(4)]
        CAR = sbt("CAR", [128, 15]); bCAR = [Buf("CAR%d" % i) for i in range(15)]
        H = sbt("H", [128, 8 * T]); bH = [Buf("H%d" % i) for i in range(8)]
        U = sbt("U", [128, 8 * T]); bU = [Buf("U%d" % i) for i in range(8)]
        YM = sbt("YM", [128, 8 * T]); bYM = [Buf("YM%d" % i) for i in range(8)]
        AR = sbt("AR", [128, 1024]); bAR = Buf("AR")
        AXT = sbt("AXT", [128, 1024]); bAXT = [Buf("AXT0"), Buf("AXT1")]
        ARb = sbt("ARb", [128, 1024]); bARb = Buf("ARb")
        AXTb = sbt("AXTb", [128, 1024]); bAXTb = [Buf("AXTb0"), Buf("AXTb1")]
        EXT = [sbt("EXT%d" % i, [128, 516]) for i in range(2)]
        bEXT0 = [Buf("EXT0_%d" % i) for i in range(2)]
        bEXT = [Buf("EXT_%d" % i) for i in range(2)]
        NW = 8
        WR = [sbt("WR%d" % i, [128, 1024]) for i in range(NW)]
        bWR = [Buf("WR%d" % i) for i in range(NW)]
        NSLAB = 36
        slabs = [Slab(sbt("SL%d" % i, [128, T]), Buf("SL%d" % i)) for i in range(NSLAB)]
        free_slabs = deque(slabs)
        PS = [pst("PS%d" % i) for i in range(8)]
        bPS = [Buf("PS%d" % i) for i in range(8)]

        def Hc(k, n=T):
            return H[:, k * T:k * T + n]

        def Uc(k, n=T):
            return U[:, k * T:k * T + n]

        def YMc(k):
            return YM[:, k * T:(k + 1) * T]

        state = {"ps": 0, "wr": 0, "ext": 0, "wq": 0}

        def alloc():
            x = free_slabs.popleft()
            state["minfree"] = min(state.get("minfree", 99), len(free_slabs))
            return x

        def free(*sl):
            for x in sl:
                free_slabs.append(x)

        def psum():
            i = state["ps"]
            state["ps"] = (i + 1) % 5
            return PS[i], bPS[i]

        ident = CON[:, C_ID:C_ID + 128]
        ones = CON[:, C_ONES:C_ONES + 128]
        bones = CON[:, C_BONES:C_BONES + 128]
        idp = CON[:, C_IDP:C_IDP + 64]
        hgm = CON[:, C_HGM:C_HGM + 128]
        rwm2 = CON[:, C_RWM2:C_RWM2 + 128]
        rwm = CON[:, C_RWM:C_RWM + 512]
        scm = CON[:, C_SCM:C_SCM + 512]

        def P(name, j=0, m=128):
            c = PCOL[name] + j
            return PAR[0:m, c:c + 1]

        def PDc(j):
            return PD[:, j:j + 1]

        def mm(out, lhsT, rhs, start, stop, r, w):
            s.add("pe", lambda h: h.matmul(out, lhsT, rhs, start=start, stop=stop), reads=r, writes=w)

        def trp(out, in_, r, w):
            s.add("pe", lambda h: h.transpose(out, in_, ident), reads=list(r) + [bCON], writes=w)

        def act(out, in_, func, r, w, bias=None, scale=None):
            kw = {}
            if bias is not None:
                kw["bias"] = bias
            if scale is not None:
                kw["scale"] = scale
            s.add("act", lambda h: h.activation(out=out, in_=in_, func=func, **kw), reads=r, writes=w)

        def acopy(out, in_, r, w):
            s.add("act", lambda h: h.copy(out, in_), reads=r, writes=w)

        def vcopy(out, in_, r, w):
            s.add("dve", lambda h: h.tensor_copy(out=out, in_=in_), reads=r, writes=w)

        def pcopy(out, in_, r, w):
            s.add("pool", lambda h: h.tensor_copy(out=out, in_=in_), reads=r, writes=w)

        def tt(out, in0, in1, op, r, w):
            s.add("dve", lambda h: h.tensor_tensor(out=out, in0=in0, in1=in1, op=op), reads=r, writes=w)

        def ts(out, in0, s1, s2, op0, op1, r, w):
            if s2 is None:
                s.add("dve", lambda h: h.tensor_scalar(out=out, in0=in0, scalar1=s1, scalar2=None, op0=op0), reads=r, writes=w)
            else:
                s.add("dve", lambda h: h.tensor_scalar(out=out, in0=in0, scalar1=s1, scalar2=s2, op0=op0, op1=op1), reads=r, writes=w)

        def stt(out, in0, scalar, in1, op0, op1, r, w):
            s.add("dve", lambda h: h.scalar_tensor_tensor(out=out, in0=in0, scalar=scalar, in1=in1, op0=op0, op1=op1), reads=r, writes=w)

        def recip(out, in_, r, w):
            s.add("dve", lambda h: h.reciprocal(out=out, in_=in_), reads=r, writes=w)

        def scan(out, d0, d1, r, w):
            s.add("dve", lambda h: h.tensor_tensor_scan(out=out, data0=d0, data1=d1, initial=0.0, op0=ALU.mult, op1=ALU.add), reads=r, writes=w)

        def dma_in(eng, out, in_, wbufs):
            s.dma(eng, lambda h: h.dma_start(out=out, in_=in_), writes=wbufs, sbuf=wbufs[0])

        def wload(src):
            i = state["wr"]
            state["wr"] = (i + 1) % NW
            q = "sp" if (state["wq"] % 2 == 0) else "pool"
            state["wq"] += 1
            dma_in(q, WR[i][:, :], src, [bWR[i]])
            return WR[i], bWR[i]

        def v3(ap, j):
            return ap.rearrange("p (s j) -> p s j", j=j)

        def dbg(name, ap, shape, rbufs):
            if debug is None or name not in debug:
                return
            o = nc.dram_tensor("dbg_" + name, list(shape), F32, kind="ExternalOutput").ap()
            dbg_out[name] = o
            s.dma("pool", lambda h: h.dma_start(out=o, in_=ap), reads=rbufs, sbuf=rbufs[0])

        def rmsnorm(src, bsrc, gname, dst, bdst, n=T):
            ps, pb = psum()
            sq = [alloc(), alloc()]
            for k in range(8):
                q = sq[k % 2]
                act(q.t[:, 0:n], src[k], AF.Square, [bsrc[k]], [q.b])
                mm(ps[:, 0:n], ones, q.t[:, 0:n], k == 0, k == 7, [bCON, q.b], [pb])
            rs = alloc()
            act(rs.t[:, 0:n], ps[:, 0:n], AF.Sqrt, [pb], [rs.b], bias=NORM_EPS, scale=1.0 / D)
            recip(rs.t[:, 0:n], rs.t[:, 0:n], [rs.b], [rs.b])
            for k in range(8):
                stt(dst[k], src[k], P(gname, k), rs.t[:, 0:n], ALU.mult, ALU.mult, [bsrc[k], bPAR, rs.b], [bdst[k]])
            free(sq[0], sq[1], rs)

        def proj(wsrc, rhs_list, brhs, m=128, n=T, nk=8):
            W, wb = wload(wsrc)
            ps, pb = psum()
            for k in range(nk):
                mm(ps[0:m, 0:n], W[:, k * 128:k * 128 + m], rhs_list[k], k == 0, k == nk - 1, [wb, brhs[k]], [pb])
            return ps, pb

        def proj_in(c):
            m = IN_M[c]
            ps, pb = proj(w_in_d[c], [Uc(k) for k in range(8)], bU, m=m)
            return ps, pb, m

        def lerp_proj(c):
            ps, pb, m = proj_in(c)
            return lerp_tail(c, ps, pb, m)

        def lerp_proj_gen(c, ps, pb):
            W, wb = wload(w_in_d[c])
            for k in range(8):
                mm(ps[:, :], W[:, k * 128:(k + 1) * 128], Uc(k), k == 0, k == 7, [wb, bU[k]], [pb])
                yield None
            yield lerp_tail(c, ps, pb, 128)

        def lerp_tail(c, ps, pb, m):
            i = state["ext"]
            state["ext"] ^= 1
            e = EXT[i]
            acopy(e[0:m, 0:1], CAR[0:m, c:c + 1], [bCAR[c]], [bEXT0[i]])
            acopy(e[0:m, 1:513], ps[0:m, :], [pb], [bEXT[i]])
            acopy(CAR[0:m, c:c + 1], e[0:m, 512:513], [bEXT[i]], [bCAR[c]])
            d = alloc()
            tt(d.t[0:m, :], e[0:m, 0:512], e[0:m, 1:513], ALU.subtract, [bEXT0[i], bEXT[i]], [d.b])
            dst = alloc()
            stt(dst.t[0:m, :], d.t[0:m, :], P("mu", c, m), e[0:m, 1:513], ALU.mult, ALU.add, [d.b, bPAR, bEXT[i]], [dst.b])
            free(d)
            return dst

        def transpose4(src_ap_fn, rb, dst_fn, wb, evac):
            ps, pb = psum()
            for stq in range(4):
                trp(ps[:, stq * 128:(stq + 1) * 128], src_ap_fn(stq), rb, [pb])
            if evac == "act":
                acopy(dst_fn(), v3(ps[:, :], 128), [pb], wb)
            else:
                vcopy(dst_fn(), v3(ps[:, :], 128), [pb], wb)

        def chk(label):
            if stop_after is not None and label == stop_after:
                raise _Stop()

        def body():
            dma_in("sp", CON[:, :], consts_d, [bCON])
            dma_in("pool", PAR[:, :], params_d, [bPAR])
            dma_in("sp", LW[:, :], lw_d, [bLW])
            dma_in("pool", G2A[:, :], g2a_d, [bG2A])
            dma_in("sp", G2B[:, :], g2b_d, [bG2B])
            s.add("pool", lambda h: h.memset(SRW[:, :], 0.0), writes=bSRW)
            s.add("pool", lambda h: h.memset(SHG[:, :], 0.0), writes=bSHG)
            s.add("pool", lambda h: h.memset(CAR[:, :], 0.0), writes=bCAR)
            dma_in("pool", SEL[:, :], sel_d, [bSEL])
            ts(PD[:, 0:4], PAR[:, PCOL["k_a"]:PCOL["k_a"] + 4], -1.0, 1.0, ALU.mult, ALU.add, [bPAR], [bPD])
            tt(PD[:, 12:16], PAR[:, PCOL["l0"]:PCOL["l0"] + 4], PAR[:, PCOL["l1"]:PCOL["l1"] + 4], ALU.subtract, [bPAR], [bPD])
            act(PD[:, 4:8], PD[:, 12:16], AF.Sigmoid, [bPD], [bPD])
            ts(PD[:, 8:12], PD[:, 4:8], -1.0, 1.0, ALU.mult, ALU.add, [bPD], [bPD])

            if n_tok > 0:
                dma_in("pool", v3(H[:, 0:8 * 256], 256), memT_d.rearrange("(k p) t -> p k t", p=128), bH)
                msrc = [H[:, k * 256:(k + 1) * 256] for k in range(8)]
                mdst = [U[:, k * 256:(k + 1) * 256] for k in range(8)]
                rmsnorm(msrc, bH, "g_mem", mdst, bU, n=256)
                for c in range(8):
                    ps, pb = proj(wk_d[c], mdst, bU, n=256)
                    acopy(KTM[:, c * 256:(c + 1) * 256], ps[:, 0:256], [pb], [bKTM])
                vps = [psum() for _ in range(4)]
                for k in range(8):
                    W, wb = wload(wv_d[k])
                    for mc in range(2):
                        for nn in range(2):
                            ps, pb = vps[mc * 2 + nn]
                            mm(ps[:, :], mdst[k][:, mc * 128:(mc + 1) * 128], W[:, nn * 512:(nn + 1) * 512], k == 0, k == 7, [bU[k], wb], [pb])
                for mc in range(2):
                    for nn in range(2):
                        ps, pb = vps[mc * 2 + nn]
                        acopy(VM[:, mc * 1024 + nn * 512: mc * 1024 + (nn + 1) * 512], ps[:, :], [pb], [bVM])

            chk('prologue')
            for ti in range(NT):
                full = ti >= n_pre
                lastpre = (ti == n_pre - 1)
                t0 = ti * T
                dma_in("pool", v3(H[:, :], T), xT_d[:, t0:t0 + T].rearrange("(k p) t -> p k t", p=128), bH)
                rmsnorm([Hc(k) for k in range(8)], bH, "g_mix", [Uc(k) for k in range(8)], bU)
                chk('norm1')

                LO1 = lerp_proj(12)
                act(LO1.t[0:64, :], LO1.t[0:64, :], AF.Tanh, [LO1.b], [LO1.b])
                chk('lo1')
                if full:
                    LG1 = lerp_proj(13)
                    act(LG1.t[:, :], LG1.t[:, :], AF.Sigmoid, [LG1.b], [LG1.b])
                    LG2 = lerp_proj(14)
                    act(LG2.t[0:32, :], LG2.t[0:32, :], AF.Sigmoid, [LG2.b], [LG2.b])
                elif lastpre:
                    free(lerp_proj(13))
                    free(lerp_proj(14))

                def fetch_hg(hh):
                    d = {}
                    if full:
                        ps, pb, _ = proj_in(15 + hh)
                        d["Q"] = alloc()
                        act(d["Q"].t[:, :], ps[:, :], AF.Silu, [pb], [d["Q"].b])
                    ps, pb, _ = proj_in(19 + hh)
                    d["Fg"] = alloc()
                    act(d["Fg"].t[:, :], ps[:, :], AF.Sigmoid, [pb], [d["Fg"].b])
                    ps, pb, _ = proj_in(23 + hh)
                    d["Ii"] = alloc()
                    acopy(d["Ii"].t[:, :], ps[:, :], [pb], [d["Ii"].b])
                    if full:
                        ps, pb, _ = proj_in(27 + hh)
                        d["GS"] = alloc()
                        act(d["GS"].t[:, :], ps[:, :], AF.Silu, [pb], [d["GS"].b])
                    return d

                extra = [Slab(Hc(k), bH[k]) for k in range(8)] + [Slab(YMc(k), bYM[k]) for k in range(4, 8)]
                free_slabs.extend(extra)
                P_ = {}
                G_ = {}

                def rw_unit(hp, R, K, V, vgen, AR, bAR, AXT, bAXT):
                    ch = slice(hp * 128, (hp + 1) * 128)
                    ps, pb = psum()
                    mm(ps[:, :], LW[0:64, ch], LO1.t[0:64, :], True, True, [bLW, LO1.b], [pb])
                    SW = alloc()
                    act(SW.t[:, :], ps[:, :], AF.Sigmoid, [pb, bPAR], [SW.b], bias=P("w0", hp))
                    ps, pb = psum()
                    mm(ps[:, :], LW[64:128, ch], LO1.t[64:128, :], True, True, [bLW, LO1.b], [pb])
                    Aa = alloc()
                    act(Aa.t[:, :], ps[:, :], AF.Sigmoid, [pb, bPAR], [Aa.b], bias=P("a0", hp))
                    if full:
                        ps, pb = psum()
                        mm(ps[:, :], G2A[:, ch], LG1.t[:, :], True, False, [bG2A, LG1.b], [pb])
                        mm(ps[:, :], G2B[0:32, ch], LG2.t[0:32, :], False, True, [bG2B, LG2.b], [pb])
                        Gt = alloc()
                        acopy(Gt.t[:, :], ps[:, :], [pb], [Gt.b])
                    KK = alloc()
                    ts(KK.t[:, :], K.t[:, :], P("k_k", hp), None, ALU.mult, None, [K.b, bPAR], [KK.b])
                    sq = alloc()
                    act(sq.t[:, :], KK.t[:, :], AF.Square, [KK.b], [sq.b])
                    ps, pb = psum()
                    mm(ps[:, :], bones, sq.t[:, :], True, True, [bCON, sq.b], [pb])
                    ts(sq.t[:, :], ps[:, :], 1e-24, None, ALU.max, None, [pb], [sq.b])
                    act(sq.t[:, :], sq.t[:, :], AF.Sqrt, [sq.b], [sq.b])
                    recip(sq.t[:, :], sq.t[:, :], [sq.b], [sq.b])
                    tt(KK.t[:, :], KK.t[:, :], sq.t[:, :], ALU.mult, [KK.b, sq.b], [KK.b])
                    KM = alloc()
                    ts(KM.t[:, :], Aa.t[:, :], P("k_a", hp), PDc(hp), ALU.mult, ALU.add, [Aa.b, bPAR, bPD], [KM.b])
                    tt(KM.t[:, :], KM.t[:, :], K.t[:, :], ALU.mult, [KM.b, K.b], [KM.b])
                    free(K)
                    BV = alloc()
                    tt(BV.t[:, :], KK.t[:, :], Aa.t[:, :], ALU.mult, [KK.b, Aa.b], [BV.b])
                    free(Aa)
                    if full:
                        stt(sq.t[:, :], R.t[:, :], P("r_k", hp), KM.t[:, :], ALU.mult, ALU.mult, [R.b, bPAR, KM.b], [sq.b])
                        ps, pb = psum()
                        mm(ps[:, :], bones, sq.t[:, :], True, True, [bCON, sq.b], [pb])
                        BON = alloc()
                        tt(BON.t[:, :], ps[:, :], V.t[:, :], ALU.mult, [pb, V.b], [BON.b])
                    free(sq)
                    CS = alloc()
                    scan(CS.t[:, :], scm, SW.t[:, :], [bCON, SW.b], [CS.b])
                    Wi = alloc()
                    act(Wi.t[:, :], CS.t[:, :], AF.Exp, [CS.b], [Wi.b], scale=-CDEC)
                    Wn = alloc()
                    act(Wn.t[:, :], CS.t[:, :], AF.Exp, [CS.b], [Wn.b], scale=CDEC)
                    tt(SW.t[:, :], CS.t[:, :], SW.t[:, :], ALU.subtract, [CS.b, SW.b], [SW.b])
                    act(SW.t[:, :], SW.t[:, :], AF.Exp, [SW.b], [SW.b], scale=-CDEC)
                    free(CS)
                    AR4 = AR[:, :].rearrange("p (s two j) -> p s two j", two=2, j=128)
                    if full:
                        tt(AR4[:, :, 1, :], v3(R.t[:, :], 128), v3(Wi.t[:, :], 128), ALU.mult, [R.b, Wi.b], [bAR])
                        free(R)
                    stt(AR4[:, :, 0, :], v3(KK.t[:, :], 128), -1.0, v3(SW.t[:, :], 128), ALU.mult, ALU.mult, [KK.b, SW.b], [bAR])
                    free(KK, SW)
                    BT = alloc()
                    tt(BT.t[:, :], BV.t[:, :], Wn.t[:, :], ALU.mult, [BV.b, Wn.b], [BT.b])
                    KT = alloc()
                    tt(KT.t[:, :], KM.t[:, :], Wn.t[:, :], ALU.mult, [KM.b, Wn.b], [KT.b])
                    free(BV, KM, Wn)
                    WCb = v3(Wi.t[:, :], 64)[:, :, 63:64].to_broadcast([128, 8, 64])
                    BH = alloc()
                    tt(v3(BH.t[:, :], 64), v3(BT.t[:, :], 64), WCb, ALU.mult, [BT.b, Wi.b], [BH.b])
                    KH = alloc()
                    tt(v3(KH.t[:, :], 64), v3(KT.t[:, :], 64), WCb, ALU.mult, [KT.b, Wi.b], [KH.b])
                    VT = alloc(); BHT = alloc(); KHT = alloc()
                    transpose4(lambda q: V.t[:, q * 128:(q + 1) * 128], [V.b], lambda: v3(VT.t[:, :], 128), [VT.b], "act")
                    transpose4(lambda q: BH.t[:, q * 128:(q + 1) * 128], [BH.b], lambda: v3(BHT.t[:, :], 128), [BHT.b], "dve")
                    transpose4(lambda q: KH.t[:, q * 128:(q + 1) * 128], [KH.b], lambda: v3(KHT.t[:, :], 128), [KHT.b], "act")
                    free(V, BH, KH)
                    AX5 = AXT[:, :].rearrange("p (s h x) -> p s h x", h=2, x=128)
                    ps, pb = psum()
                    for q in range(4):
                        trp(ps[:, q * 128:(q + 1) * 128], AR4[:, q, 0, :], [bAR], [pb])
                    psv = ps[:, :].rearrange("p (s h k) -> p s h k", h=2, k=64)
                    vcopy(AX5[:, :, 0, 0:64], psv[:, :, 0, :], [pb], [bAXT[0]])
                    vcopy(AX5[:, :, 1, 0:64], psv[:, :, 1, :], [pb], [bAXT[1]])
                    chk('front')
                    yield None

                    MA = [[None] * 4 for _ in range(2)]
                    AVP = [None, None]
                    ncol = 256 if full else 128
                    psR, pbR = PS[5], bPS[5]
                    psGa, pbGa = PS[6], bPS[6]
                    psGb, pbGb = PS[7], bPS[7]
                    for hd in range(2):
                        pq = slice(hd * 64, hd * 64 + 64)
                        for q in range(4):
                            tc = slice(q * 128, (q + 1) * 128)
                            ps, pb = psum()
                            if full:
                                mm(ps[:, 0:256], BT.t[pq, tc], AR[pq, q * 256:(q + 1) * 256], True, True, [BT.b, bAR], [pb])
                                mm(ps[:, 256:512], KT.t[pq, tc], AR[pq, q * 256:(q + 1) * 256], True, True, [KT.b, bAR], [pb])
                                m_ = alloc()
                                tt(m_.t[:, :], ps[:, :], rwm, ALU.mult, [pb, bCON], [m_.b])
                            else:
                                mm(ps[:, 0:128], BT.t[pq, tc], AR[pq, q * 256:q * 256 + 128], True, True, [BT.b, bAR], [pb])
                                mm(ps[:, 256:384], KT.t[pq, tc], AR[pq, q * 256:q * 256 + 128], True, True, [KT.b, bAR], [pb])
                                m_ = alloc()
                                tt(m_.t[:, 0:128], ps[:, 0:128], rwm[:, 0:128], ALU.mult, [pb, bCON], [m_.b])
                                tt(m_.t[:, 256:384], ps[:, 256:384], rwm[:, 256:384], ALU.mult, [pb, bCON], [m_.b])
                            MA[hd][q] = m_
                        chk('amat')
                        ps, pb = psum()
                        for q in range(4):
                            mm(ps[:, q * 128:(q + 1) * 128], AR[pq, q * 256:q * 256 + 128], BT.t[pq, q * 128:(q + 1) * 128], True, True, [bAR, BT.b], [pb])
                        Pa = alloc()
                        tt(v3(Pa.t[:, :], 128), v3(ps[:, :], 128), rwm2.unsqueeze(1).to_broadcast([128, 4, 128]), ALU.mult, [pb, bCON], [Pa.b])
                        Xa = alloc()
                        for q in range(4):
                            tt(Xa.t[:, q * 128:(q + 1) * 128], MA[hd][q].t[:, 0:128], ident, ALU.add, [MA[hd][q].b, bCON], [Xa.b])
                        Pcur = [Pa.t[:, q * 128:(q + 1) * 128] for q in range(4)]
                        PTcur = [MA[hd][q].t[:, 0:128] for q in range(4)]
                        bP = [Pa.b] * 4
                        bPT = [MA[hd][q].b for q in range(4)]
                        held = [Pa]
                        for lvl in range(1, 6):
                            psL, pbL = psum()
                            for q in range(4):
                                mm(psL[:, q * 128:(q + 1) * 128], PTcur[q], Pcur[q], True, True, [bPT[q], bP[q]], [pbL])
                            Pn = alloc()
                            acopy(Pn.t[:, :], psL[:, :], [pbL], [Pn.b])
                            if lvl < 5:
                                psT, pbT = psum()
                                for q in range(4):
                                    mm(psT[:, q * 128:(q + 1) * 128], Pcur[q], PTcur[q], True, True, [bP[q], bPT[q]], [pbT])
                                PTn = alloc()
                                vcopy(PTn.t[:, :], psT[:, :], [pbT], [PTn.b])
                            psU, pbU = psum()
                            for q in range(4):
                                mm(psU[:, q * 128:(q + 1) * 128], Pn.t[:, q * 128:(q + 1) * 128], Xa.t[:, q * 128:(q + 1) * 128], True, True, [Pn.b, Xa.b], [pbU])
                            Xn = alloc()
                            tt(Xn.t[:, :], psU[:, :], Xa.t[:, :], ALU.add, [pbU, Xa.b], [Xn.b])
                            free(Xa)
                            Xa = Xn
                            free(*held)
                            held = [Pn]
                            Pcur = [Pn.t[:, q * 128:(q + 1) * 128] for q in range(4)]
                            bP = [Pn.b] * 4
                            if lvl < 5:
                                held.append(PTn)
                                PTcur = [PTn.t[:, q * 128:(q + 1) * 128] for q in range(4)]
                                bPT = [PTn.b] * 4
                        free(*held)
                        chk('chain')
                        ps, pb = psum()
                        VT3 = v3(VT.t[:, :], 128)
                        for q in range(4):
                            mm(ps[:, q * 64:(q + 1) * 64], MA[hd][q].t[:, 256:384], VT3[:, q, pq], True, True, [MA[hd][q].b, VT.b], [pb])
                        acopy(AX5[:, :, hd, 64:128], v3(ps[:, 0:256], 64), [pb], [bAXT[hd]])
                        ps, pb = psum()
                        for q in range(4):
                            mm(ps[:, q * 128:(q + 1) * 128], Xa.t[:, q * 128:(q + 1) * 128], AX5[:, q, hd, :], True, True, [Xa.b, bAXT[hd]], [pb])
                        av = alloc()
                        acopy(av.t[:, :], ps[:, :], [pb], [av.b])
                        AVP[hd] = av
                        free(Xa)
                        av3 = v3(av.t[:, :], 128)
                        chk('xav')
                        if full:
                            for q in range(4):
                                tc = slice(q * 128, (q + 1) * 128)
                                mm(psR[pq, tc], av3[:, q, 0:64], MA[hd][q].t[:, 128:256], True, False, [av.b, MA[hd][q].b], [pbR])
                                mm(psR[pq, tc], CON[pq, C_ID + hd * 64:C_ID + hd * 64 + 64], AR[pq, q * 256 + 128:(q + 1) * 256], False, True, [bCON, bAR], [pbR])
                        chk('rp')
                        BHT3 = v3(BHT.t[:, :], 128)
                        KHT3 = v3(KHT.t[:, :], 128)
                        for c in range(8):
                            q = c // 2
                            tb = slice((c % 2) * 64, (c % 2) * 64 + 64)
                            bank, bb = (psGa, pbGa) if c % 2 == 0 else (psGb, pbGb)
                            c0 = (c // 2) * 128
                            mm(bank[pq, c0:c0 + 64], av3[tb, q, 0:64], BHT3[tb, q, pq], True, True, [av.b, BHT.b], [bb])
                            mm(bank[pq, c0 + 64:c0 + 128], BHT3[tb, q, pq], av3[tb, q, 64:128], True, False, [av.b, BHT.b], [bb])
                            mm(bank[pq, c0 + 64:c0 + 128], KHT3[tb, q, pq], VT3[tb, q, pq], False, True, [KHT.b, VT.b], [bb])
                    free(BT, KT, BHT, KHT)
                    chk('gh')
                    GH = [alloc(), alloc()]
                    acopy(GH[0].t[:, :], psGa[:, :], [pbGa], [GH[0].b])
                    acopy(GH[1].t[:, :], psGb[:, :], [pbGb], [GH[1].b])
                    for c in range(8):
                        g = GH[c % 2]
                        gv = g.t[:, (c // 2) * 128:(c // 2) * 128 + 64]
                        stt(gv, idp, Wi.t[:, c * 64 + 63:c * 64 + 64], gv, ALU.mult, ALU.add, [bCON, Wi.b, g.b], [g.b])
                    free(Wi)
                    chk('ghe')
                    if full:
                        RP = alloc()
                        acopy(RP.t[:, :], psR[:, :], [pbR], [RP.b])
                    STT = alloc()
                    pcopy(STT.t[:, 0:64], SRW[:, hp * 64:(hp + 1) * 64], [bSRW[hp]], [STT.b])
                    chk('stcopy')
                    for c in range(8):
                        g = GH[c % 2]
                        c0 = (c // 2) * 128
                        for hd in range(2):
                            pq = slice(hd * 64, hd * 64 + 64)
                            ps, pb = psum()
                            mm(ps[pq, 0:64], g.t[pq, c0:c0 + 64], STT.t[pq, c * 64:(c + 1) * 64], True, True, [g.b, STT.b], [pb])
                            if c < 7:
                                tt(STT.t[pq, (c + 1) * 64:(c + 2) * 64], ps[pq, 0:64], g.t[pq, c0 + 64:c0 + 128], ALU.add, [pb, g.b], [STT.b])
                            else:
                                tt(SRW[pq, hp * 64:(hp + 1) * 64], ps[pq, 0:64], g.t[pq, c0 + 64:c0 + 128], ALU.add, [pb, g.b], [bSRW[hp]])
                        if vgen is not None:
                            next(vgen)
                    if vgen is not None:
                        P_[hp + 2]["V"] = next(vgen)
                    free(GH[0], GH[1])
                    chk('state')
                    if full:
                        psY, pbY = psum()
                        for hd in range(2):
                            pq = slice(hd * 64, hd * 64 + 64)
                            av3 = v3(AVP[hd].t[:, :], 128)
                            for q in range(4):
                                c0, c1 = 2 * q, 2 * q + 1
                                mm(psY[pq, c0 * 64:c0 * 64 + 64], STT.t[pq, c0 * 64:c0 * 64 + 64], RP.t[pq, c0 * 64:c0 * 64 + 64], True, False, [STT.b, RP.b], [pbY])
                                mm(psY[pq, c1 * 64:c1 * 64 + 64], STT.t[pq, c1 * 64:c1 * 64 + 64], RP.t[pq, c1 * 64:c1 * 64 + 64], False, False, [STT.b, RP.b], [pbY])
                                mm(psY[pq, q * 128:(q + 1) * 128], av3[:, q, 64:128], MA[hd][q].t[:, 128:256], False, False, [AVP[hd].b, MA[hd][q].b], [pbY])
                                mm(psY[pq, q * 128:(q + 1) * 128], VT3[:, q, pq], MA[hd][q].t[:, 384:512], False, True, [VT.b, MA[hd][q].b], [pbY])
                        free(RP)
                    free(STT, VT, AVP[0], AVP[1])
                    for hd in range(2):
                        free(*MA[hd])
                    if full:
                        Y = alloc()
                        acopy(Y.t[:, :], psY[:, :], [pbY], [Y.b])
                        if hp == 0:
                            dbg("y0", Y.t[:, :], [128, 512], [Y.b])
                        ps, pb = psum()
                        mm(ps[:, :], bones, Y.t[:, :], True, True, [bCON, Y.b], [pb])
                        psg, pbg, _ = proj_in(31 + hp)
                        GA = alloc()
                        act(GA.t[:, :], psg[:, :], AF.Sigmoid, [pbg], [GA.b])
                        stt(Y.t[:, :], ps[:, :], -1.0 / 64, Y.t[:, :], ALU.mult, ALU.add, [pb, Y.b], [Y.b])
                        sq = alloc()
                        act(sq.t[:, :], Y.t[:, :], AF.Square, [Y.b], [sq.b])
                        ps, pb = psum()
                        mm(ps[:, :], bones, sq.t[:, :], True, True, [bCON, sq.b], [pb])
                        act(sq.t[:, :], ps[:, :], AF.Sqrt, [pb], [sq.b], bias=RW_LN_EPS, scale=1.0 / 64)
                        recip(sq.t[:, :], sq.t[:, :], [sq.b], [sq.b])
                        tt(Y.t[:, :], Y.t[:, :], sq.t[:, :], ALU.mult, [Y.b, sq.b], [Y.b])
                        ts(Y.t[:, :], Y.t[:, :], P("ln_w", hp), P("ln_b", hp), ALU.mult, ALU.add, [Y.b, bPAR], [Y.b])
                        tt(Y.t[:, :], Y.t[:, :], BON.t[:, :], ALU.add, [Y.b, BON.b], [Y.b])
                        tt(Y.t[:, :], Y.t[:, :], Gt.t[:, :], ALU.mult, [Y.b, Gt.b], [Y.b])
                        tt(YMc(hp), Y.t[:, :], GA.t[:, :], ALU.mult, [Y.b, GA.b], [bYM[hp]])
                        free(GA)
                        if hp == 0:
                            dbg("ya0", YMc(0), [128, 512], [bYM[0]])
                        free(Y, sq, BON, Gt)
                    chk('pair0')

                def hg_unit(hh, cur):
                    Q = cur.get("Q")
                    Fg = cur["Fg"]
                    ts(Fg.t[:, :], Fg.t[:, :], PDc(8 + hh), PDc(4 + hh), ALU.mult, ALU.add, [Fg.b, bPD], [Fg.b])
                    LF = alloc()
                    act(LF.t[:, :], Fg.t[:, :], AF.Ln, [Fg.b], [LF.b])
                    ts(Fg.t[:, :], Fg.t[:, :], -1.0, 1.0, ALU.mult, ALU.add, [Fg.b], [Fg.b])
                    Ii = cur["Ii"]
                    IT = alloc()
                    transpose4(lambda q: Ii.t[:, q * 128:(q + 1) * 128], [Ii.b], lambda: v3(IT.t[:, :], 128), [IT.b], "dve")
                    free(Ii)
                    GS = cur.get("GS")
                    Bc = alloc()
                    scan(Bc.t[:, :], scm, LF.t[:, :], [bCON, LF.b], [Bc.b])
                    free(LF)
                    B3 = v3(Bc.t[:, :], 64)
                    E3 = alloc()
                    act(E3.t[:, :], Bc.t[:, :], AF.Exp, [Bc.b], [E3.b])
                    if full:
                        BM = alloc()
                        tt(v3(BM.t[:, :], 64), B3, B3[:, :, 31:32].to_broadcast([128, 8, 64]), ALU.subtract, [Bc.b], [BM.b])
                        E1 = alloc()
                        act(E1.t[:, :], BM.t[:, :], AF.Exp, [BM.b], [E1.b])
                        act(BM.t[:, :], BM.t[:, :], AF.Exp, [BM.b], [BM.b], scale=-1.0)
                        tt(E1.t[:, :], E1.t[:, :], Q.t[:, :], ALU.mult, [E1.b, Q.b], [E1.b])
                        tt(BM.t[:, :], BM.t[:, :], Fg.t[:, :], ALU.mult, [BM.b, Fg.b], [BM.b])
                        QT, KTh = E1, BM
                        tt(Q.t[:, :], Q.t[:, :], E3.t[:, :], ALU.mult, [Q.b, E3.b], [Q.b])
                    BE = alloc()
                    tt(v3(BE.t[:, :], 64), B3, B3[:, :, 63:64].to_broadcast([128, 8, 64]), ALU.subtract, [Bc.b], [BE.b])
                    free(Bc)
                    act(BE.t[:, :], BE.t[:, :], AF.Exp, [BE.b], [BE.b], scale=-1.0)
                    tt(BE.t[:, :], BE.t[:, :], Fg.t[:, :], ALU.mult, [BE.b, Fg.b], [BE.b])
                    free(Fg)
                    KHT = alloc()
                    transpose4(lambda q: BE.t[:, q * 128:(q + 1) * 128], [BE.b], lambda: v3(KHT.t[:, :], 128), [KHT.b], "act")
                    free(BE)
                    IT3 = v3(IT.t[:, :], 128)
                    KHT3 = v3(KHT.t[:, :], 128)
                    yield None
                    HH = [alloc(), alloc()]
                    for bi in range(2):
                        ps, pb = psum()
                        tb = slice(bi * 64, bi * 64 + 64)
                        for q in range(4):
                            mm(ps[:, q * 128:(q + 1) * 128], KHT3[tb, q, :], IT3[tb, q, :], True, True, [KHT.b, IT.b], [pb])
                        acopy(HH[bi].t[:, :], ps[:, :], [pb], [HH[bi].b])
                    free(KHT)
                    SS = [alloc(), alloc()]

                    def SSc(c):
                        return SS[c // 4].t[:, (c % 4) * 128:(c % 4 + 1) * 128]

                    pcopy(SSc(0), SHG[:, hh * 128:(hh + 1) * 128], [bSHG[hh]], [SS[0].b])
                    for c in range(8):
                        dE = E3.t[:, c * 64 + 63:c * 64 + 64]
                        hsrc = HH[c % 2].t[:, (c // 2) * 128:(c // 2 + 1) * 128]
                        if c < 7:
                            stt(SSc(c + 1), SSc(c), dE, hsrc, ALU.mult, ALU.add, [SS[c // 4].b, E3.b, HH[c % 2].b], [SS[(c + 1) // 4].b])
                        else:
                            stt(SHG[:, hh * 128:(hh + 1) * 128], SSc(c), dE, hsrc, ALU.mult, ALU.add, [SS[1].b, E3.b, HH[c % 2].b], [bSHG[hh]])
                    free(E3, HH[0], HH[1])
                    if full:
                        ps, pb = psum()
                        for q in range(4):
                            tc = slice(q * 128, (q + 1) * 128)
                            mm(ps[:, tc], KTh.t[:, tc], QT.t[:, tc], True, True, [KTh.b, QT.b], [pb])
                        SM = alloc()
                        tt(v3(SM.t[:, :], 128), v3(ps[:, :], 128), hgm.unsqueeze(1).to_broadcast([128, 4, 128]), ALU.mult, [pb, bCON], [SM.b])
                        free(QT, KTh)
                        psO, pbO = psum()
                        for q in range(4):
                            c0, c1 = 2 * q, 2 * q + 1
                            mm(psO[:, c0 * 64:c0 * 64 + 64], SSc(c0), Q.t[:, c0 * 64:c0 * 64 + 64], True, False, [SS[c0 // 4].b, Q.b], [pbO])
                            mm(psO[:, c1 * 64:c1 * 64 + 64], SSc(c1), Q.t[:, c1 * 64:c1 * 64 + 64], False, False, [SS[c1 // 4].b, Q.b], [pbO])
                            mm(psO[:, q * 128:(q + 1) * 128], IT3[:, q, :], SM.t[:, q * 128:(q + 1) * 128], False, True, [IT.b, SM.b], [pbO])
                        free(Q)
                        sq = SM
                        act(sq.t[:, :], psO[:, :], AF.Square, [pbO], [sq.b])
                        psg, pbg, _ = proj_in(35 + hh)
                        GB = alloc()
                        act(GB.t[:, :], psg[:, :], AF.Sigmoid, [pbg], [GB.b])
                        ps, pb = psum()
                        mm(ps[:, :], ones, sq.t[:, :], True, True, [bCON, sq.b], [pb])
                        act(sq.t[:, :], ps[:, :], AF.Sqrt, [pb], [sq.b], bias=NORM_EPS, scale=1.0 / 128)
                        recip(sq.t[:, :], sq.t[:, :], [sq.b], [sq.b])
                        O = alloc()
                        tt(O.t[:, :], psO[:, :], sq.t[:, :], ALU.mult, [pbO, sq.b], [O.b])
                        stt(O.t[:, :], O.t[:, :], P("hg_g"), GS.t[:, :], ALU.mult, ALU.mult, [O.b, bPAR, GS.b], [O.b])
                        tt(O.t[:, :], O.t[:, :], GB.t[:, :], ALU.mult, [O.b, GB.b], [O.b])
                        free(GB)
                        if hh == 0:
                            dbg("yb0", O.t[:, :], [128, 512], [O.b])
                        tt(YMc(hh), YMc(hh), O.t[:, :], ALU.add, [bYM[hh], O.b], [bYM[hh]])
                        free(O, sq, GS)
                    free(IT, SS[0], SS[1])
                    chk('hg0')

                ARS = [(AR, bAR, AXT, bAXT), (ARb, bARb, AXTb, bAXTb)]

                def fetch(u):
                    if u < 4:
                        return {"R": lerp_proj(u), "K": lerp_proj(4 + u), "V": None}
                    return fetch_hg(u - 4)

                def start(u):
                    if u < 4:
                        vg = lerp_proj_gen(8 + u + 2, PS[5], bPS[5]) if u + 2 < 4 else None
                        g = rw_unit(u, P_[u]["R"], P_[u]["K"], P_[u]["V"], vg, *ARS[u % 2])
                    else:
                        g = hg_unit(u - 4, P_[u])
                    next(g)
                    G_[u] = g

                P_[0] = fetch(0)
                P_[0]["V"] = lerp_proj(8)
                P_[1] = fetch(1)
                P_[1]["V"] = lerp_proj(9)
                start(0)
                for u in range(8):
                    if u + 2 < 8:
                        P_[u + 2] = fetch(u + 2)
                    if u + 1 < 8:
                        start(u + 1)
                    for _ in G_[u]:
                        pass
                free(LO1, LG1, LG2)
                for x in extra:
                    free_slabs.remove(x)

                s.dma("sp", lambda h, ti=ti: h.dma_start(out=yloc_t[ti].ap().rearrange("(k p) t -> p k t", p=128), in_=v3(YM[:, 0:4 * T], T)),
                      reads=bYM[0:4], writes=[bYL[ti]], sbuf=bYM[0])
                groups = [[2 * g, 2 * g + 1] for g in range(n_cores // 2)]
                s.coll(lambda h, ti=ti: h.collective_compute("AllGather", ALU.bypass, replica_groups=groups,
                                                             ins=[yloc_t[ti].ap().opt()], outs=[yall_t[ti].ap().opt()]),
                       reads=[bYL[ti]], writes=[bYALL[ti]], sbuf=bCC[ti])

            for ti in range(n_tok):
                t0 = ti * T
                dma_in("pool", v3(H[:, :], T), xtok_d[:, t0:t0 + T].rearrange("(k p) t -> p k t", p=128), bH)
                s.dma("pool", lambda h, ti=ti: h.dma_start(out=v3(U[:, :], T), in_=yall_t[ti].ap().rearrange("(k p) t -> p k t", p=128)),
                      reads=[bYALL[ti]], writes=bU, sbuf=bU[0])
                YB = [alloc() for _ in range(8)]
                for k in range(8):
                    s.dma("sp", lambda h, ti=ti, k=k, dst=YB[k].t: h.dma_start(out=dst[:, :], in_=yall_t[n_tok + ti].ap()[k * 128:(k + 1) * 128, :]),
                          reads=[bYALL[n_tok + ti]], writes=[YB[k].b], sbuf=YB[k].b)
                for k in range(8):
                    ts(YMc(k), Uc(k), SEL[:, 0:1], None, ALU.mult, None, [bU[k], bSEL], [bYM[k]])
                    stt(YMc(k), YB[k].t[:, :], SEL[:, 1:2], YMc(k), ALU.mult, ALU.add, [YB[k].b, bSEL, bYM[k]], [bYM[k]])
                free(*YB)

                for oc in range(8):
                    ps, pb = proj(w_out_d[oc], [YMc(k) for k in range(8)], bYM)
                    tt(Hc(oc), Hc(oc), ps[:, :], ALU.add, [bH[oc], pb], [bH[oc]])
                if ti == 0:
                    dbg("h1", H[:, 0:T], [128, T], [bH[0]])

                rmsnorm([Hc(k) for k in range(8)], bH, "g_xa", [Uc(k) for k in range(8)], bU)
                QX = []
                for c in range(8):
                    ps, pb = proj(wq_d[c], [Uc(k) for k in range(8)], bU)
                    qx = alloc()
                    acopy(qx.t[:, :], ps[:, :], [pb], [qx.b])
                    QX.append(qx)
                OX = []
                for a in range(4):
                    Es = []
                    for mc in range(2):
                        ps, pb = psum()
                        for j in range(2):
                            c = 2 * a + j
                            mm(ps[:, :], KTM[:, c * 256 + mc * 128:c * 256 + (mc + 1) * 128], QX[c].t[:, :], j == 0, j == 1, [bKTM, QX[c].b], [pb])
                        e = alloc()
                        act(e.t[:, :], ps[:, :], AF.Exp, [pb], [e.b], scale=1.0 / 16)
                        Es.append(e)
                    ps, pb = psum()
                    mm(ps[:, :], ones, Es[0].t[:, :], True, False, [bCON, Es[0].b], [pb])
                    mm(ps[:, :], ones, Es[1].t[:, :], False, True, [bCON, Es[1].b], [pb])
                    rd = alloc()
                    recip(rd.t[:, :], ps[:, :], [pb], [rd.b])
                    for j in range(2):
                        c = 2 * a + j
                        ps, pb = psum()
                        mm(ps[:, :], VM[:, c * 128:(c + 1) * 128], Es[0].t[:, :], True, False, [bVM, Es[0].b], [pb])
                        mm(ps[:, :], VM[:, 1024 + c * 128:1024 + (c + 1) * 128], Es[1].t[:, :], False, True, [bVM, Es[1].b], [pb])
                        ox = alloc()
                        tt(ox.t[:, :], ps[:, :], rd.t[:, :], ALU.mult, [pb, rd.b], [ox.b])
                        OX.append(ox)
                    free(Es[0], Es[1], rd)
                free(*QX)
                for oc in range(8):
                    ps, pb = proj(wo_d[oc], [OX[k].t[:, :] for k in range(8)], [OX[k].b for k in range(8)])
                    tt(Hc(oc), Hc(oc), ps[:, :], ALU.add, [bH[oc], pb], [bH[oc]])
                free(*OX)
                if ti == 0:
                    dbg("h2", H[:, 0:T], [128, T], [bH[0]])

                rmsnorm([Hc(k) for k in range(8)], bH, "g_ffn", [Uc(k) for k in range(8)], bU)
                HID = []
                for j in range(22):
                    ps1, pb1 = proj(w1_d[j], [Uc(k) for k in range(8)], bU)
                    ps3, pb3 = proj(w3_d[j], [Uc(k) for k in range(8)], bU)
                    hj = alloc()
                    act(hj.t[:, :], ps1[:, :], AF.Silu, [pb1], [hj.b])
                    tt(hj.t[:, :], hj.t[:, :], ps3[:, :], ALU.mult, [hj.b, pb3], [hj.b])
                    HID.append(hj)
                for oc in range(8):
                    ps, pb = psum()
                    for g in range(3):
                        W, wb = wload(w2_d[oc * 3 + g])
                        nk = 8 if g < 2 else 6
                        for kk in range(nk):
                            kidx = g * 8 + kk
                            mm(ps[:, :], W[:, kk * 128:(kk + 1) * 128], HID[kidx].t[:, :], kidx == 0, kidx == 21, [wb, HID[kidx].b], [pb])
                    tt(Hc(oc), Hc(oc), ps[:, :], ALU.add, [bH[oc], pb], [bH[oc]])
                free(*HID)

                rmsnorm([Hc(k) for k in range(8)], bH, "g_fin", [Uc(k) for k in range(8)], bU)
                o0 = ti * T
                s.dma("pool", lambda h, o0=o0: h.dma_start(out=outT_d[:, o0:o0 + T].rearrange("(k p) t -> p k t", p=128), in_=v3(U[:, :], T)),
                      reads=bU, sbuf=bU[0])

        try:
            body()
        except _Stop:
            pass
        s.emit()
    nc._minfree = state.get('minfree')
    nc._sched_stats = {e: (len(s.ops[e]), sum(1 for o in s.ops[e] if o.signal)) for e in ENGS}
    return nc, dbg_out


def _chunk_w(w, ncol_chunks=None, chunks=None):
    K, N = w.shape
    nk = K // 128
    if chunks is None:
        chunks = [(c * 128, 128) for c in range(N // 128)]
    out = np.zeros((len(chunks), 128, nk * 128), np.float32)
    w3 = w.reshape(nk, 128, N)
    for i, (c0, m) in enumerate(chunks):
        blk = w3[:, :, c0:c0 + m]
        o = out[i].reshape(128, nk, 128)
        o[:, :, :m] = blk.transpose(1, 0, 2)
    return out


def _vec_cols(v):
    v = np.asarray(v, np.float32).reshape(-1)
    n = v.shape[0] // 128
    return np.ascontiguousarray(v.reshape(n, 128).T)


def make_consts():
    c = np.zeros((128, NCONST), np.float32)
    idx = np.arange(128)
    c[:, C_ID:C_ID + 128] = np.eye(128)
    c[:, C_ONES:C_ONES + 128] = 1.0
    same = (idx[:, None] // 64) == (idx[None, :] // 64)
    c[:, C_BONES:C_BONES + 128] = same
    c[:, C_IDP:C_IDP + 64] = (idx[:, None] % 64) == np.arange(64)[None, :]
    le = idx[:, None] <= idx[None, :]
    lt = idx[:, None] < idx[None, :]
    c[:, C_HGM:C_HGM + 128] = same & le
    c[:, C_RWM2:C_RWM2 + 128] = same & (idx[:, None] > idx[None, :])
    c[:, C_RWM:C_RWM + 512] = np.concatenate([same & lt, same & le, same & lt, same & le], 1)
    sm = np.ones(512, np.float32)
    sm[::64] = 0.0
    c[:, C_SCM:C_SCM + 512] = sm[None, :]
    return c


def prep_shared(inp):
    f = lambda k: np.asarray(inp[k], np.float32)
    w2 = f("ffn_w2")[0]
    w2p = np.zeros((24 * 128, 1024), np.float32)
    w2p[:2816] = w2
    w2c = _chunk_w(w2p)
    w2g = np.ascontiguousarray(w2c.reshape(8, 128, 3, 1024).transpose(0, 2, 1, 3).reshape(24, 128, 1024))
    return {
        "consts": make_consts(),
        "w_out": _chunk_w(f("w_out")[0]),
        "wq": _chunk_w(f("xa_wq")[0]),
        "wk": _chunk_w(f("xa_wk")[0]),
        "wv": np.ascontiguousarray(f("xa_wv")[0].reshape(8, 128, 1024)),
        "wo": _chunk_w(f("xa_wo")[0]),
        "w1": _chunk_w(f("ffn_w1")[0]),
        "w3": _chunk_w(f("ffn_w3")[0]),
        "w2": w2g,
    }


def prep_half(inp, half):
    f = lambda k: np.asarray(inp[k], np.float32)
    par = np.zeros((128, NPAR), np.float32)
    own = slice(4 * half, 4 * half + 4)

    def put(name, v, sel=None):
        cols = _vec_cols(v)
        if sel is not None:
            cols = cols[:, sel]
        par[:, PCOL[name]:PCOL[name] + cols.shape[1]] = cols

    chunks = own_chunks(half)
    put("g_mix", f("norm_mix_g")[0])
    mu_full = f("rw_mu")[0]
    for c in range(15):
        c0, m = chunks[c]
        par[:m, PCOL["mu"] + c] = mu_full[c0:c0 + m]
    for name, key in (("w0", "rw_w0"), ("a0", "rw_a0"), ("k_k", "rw_k_k"), ("k_a", "rw_k_a"), ("ln_w", "rw_ln_w"), ("ln_b", "rw_ln_b")):
        put(name, f(key)[0], own)
    put("r_k", f("rw_r_k")[0].reshape(-1), own)
    put("l0", f("hg_lb_logits")[0], own)
    put("l1", f("hg_lb_logits")[1], own)
    par[:, PCOL["hg_g"]] = f("hg_norm_g")[0]
    put("g_xa", f("norm_xa_g")[0]); put("g_mem", f("norm_mem_g")[0]); put("g_ffn", f("norm_ffn_g")[0])
    put("g_fin", f("norm_final_g"))
    cs = slice(512 * half, 512 * half + 512)
    sel = np.zeros((128, 2), np.float32)
    sel[:, half] = 1.0
    return {
        "params": par,
        "w_in": _chunk_w(f("w_in")[0], chunks=chunks),
        "lw": np.ascontiguousarray(np.concatenate([f("rw_w2")[0], f("rw_a2")[0]], 0)[:, cs]),
        "g2a": np.ascontiguousarray(f("rw_g2")[0][:128, cs]),
        "g2b": np.ascontiguousarray(f("rw_g2")[0][128:160, cs]),
        "sel": sel,
    }


N_MIX = 16
N_TOK = 8


def kernel(**inputs):
    x = np.asarray(inputs["x"], np.float32)
    mem = np.asarray(inputs["mem"], np.float32)
    B, S, _ = x.shape
    sh = prep_shared(inputs)
    hv = [prep_half(inputs, 0), prep_half(inputs, 1)]
    half = S // 2
    in_maps = []
    for core in range(8):
        b, hf = core // 2, core % 2
        m = dict(sh)
        m.update(hv[hf])
        xt = np.ascontiguousarray(x[b].T)
        m["xT"] = xt
        m["xtok"] = np.ascontiguousarray(xt[:, hf * half:(hf + 1) * half])
        m["memT"] = np.ascontiguousarray(mem[b].T)
        in_maps.append(m)
    nc, _ = build_program(N_MIX, N_TOK, 8)
    res = run_bass_kernel_spmd(nc, in_maps, core_ids=list(range(8)))
    out = np.empty((B, S, D), np.float32)
    for core in range(8):
        b, hf = core // 2, core % 2
        out[b, hf * half:(hf + 1) * half] = res.results[core]["outT"].T
    return out
```

```python
import numpy as np
from collections import deque
from contextlib import ExitStack
import concourse.bass as bass
import concourse.mybir as mybir
from concourse.bass_utils import run_bass_kernel_spmd

F32 = mybir.dt.float32
ALU = mybir.AluOpType
AF = mybir.ActivationFunctionType
ENGS = ("pe", "act", "dve", "pool", "sp")

D = 1024
T = 512
CDEC = float(np.exp(-0.5))
NORM_EPS = 1e-6
RW_LN_EPS = 64e-5

IN_M = [128] * 14 + [32] + [128] * 24


def own_chunks(half):
    hs = [4 * half + j for j in range(4)]
    ch = [(h * 128, 128) for h in hs] + [(1024 + h * 128, 128) for h in hs] + [(2048 + h * 128, 128) for h in hs]
    ch += [(3072, 128), (3200, 128), (3328, 32)]
    for base in (3360, 4384, 5408, 6432, 7456, 8480):
        ch += [(base + h * 128, 128) for h in hs]
    return ch
C_ID, C_ONES, C_BONES, C_IDP, C_HGM, C_RWM2, C_RWM, C_SCM, NCONST = 0, 128, 256, 384, 448, 576, 704, 1216, 1728
PCOL = {}
_o = 0
for _n, _w in [("g_mix", 8), ("mu", 15), ("w0", 4), ("a0", 4), ("k_k", 4), ("k_a", 4), ("r_k", 4), ("ln_w", 4),
               ("ln_b", 4), ("l0", 4), ("l1", 4), ("hg_g", 1), ("g_xa", 8), ("g_mem", 8), ("g_ffn", 8), ("g_fin", 8)]:
    PCOL[_n] = _o
    _o += _w
NPAR = _o


class Buf:
    __slots__ = ("name", "writers", "readers", "dma_sem", "dma_cnt")

    def __init__(self, name=""):
        self.name = name
        self.writers = []
        self.readers = []
        self.dma_sem = None
        self.dma_cnt = 0


class Op:
    __slots__ = ("eng", "fn", "idx", "waits", "signal", "semval", "kind", "dsem", "dval", "cc")

    def __init__(self, eng, fn, idx, kind):
        self.eng, self.fn, self.idx, self.kind = eng, fn, idx, kind
        self.waits = []
        self.signal = False
        self.semval = 0
        self.dsem = None
        self.dval = 0
        self.cc = False


class Sched:
    def __init__(self, nc):
        self.nc = nc
        self.ops = {e: [] for e in ENGS}
        self.waited = {e: {} for e in ENGS}
        self.dma_bufs = []

    def _dep(self, op, d):
        if d is op:
            return
        w = self.waited[op.eng]
        if d.kind == "d":
            key = ("d", id(d.dsem))
            if w.get(key, -1) >= d.dval:
                return
            w[key] = d.dval
            op.waits.append(d)
            return
        if d.eng == op.eng and op.kind == "c":
            if op.eng == "pe":
                return
        key = ("e", d.eng)
        if w.get(key, -1) >= d.idx:
            return
        w[key] = d.idx
        d.signal = True
        op.waits.append(d)

    def _mk(self, eng, fn, reads, writes, kind):
        op = Op(eng, fn, len(self.ops[eng]), kind)
        for b in reads:
            for d in b.writers:
                self._dep(op, d)
        for b in writes:
            for d in b.writers:
                self._dep(op, d)
            for d in b.readers:
                self._dep(op, d)
        for b in reads:
            b.readers.append(op)
        for b in writes:
            b.writers = [op]
            b.readers = []
        self.ops[eng].append(op)
        return op

    def add(self, eng, fn, reads=(), writes=()):
        return self._mk(eng, fn, reads, writes, "c")

    def dma(self, eng, fn, reads=(), writes=(), sbuf=None):
        op = self._mk(eng, fn, reads, writes, "d")
        if sbuf.dma_sem is None:
            sbuf.dma_sem = "pending"
            self.dma_bufs.append(sbuf)
        sbuf.dma_cnt += 16
        op.dsem = sbuf
        op.dval = sbuf.dma_cnt
        return op

    def coll(self, fn, reads, writes, sbuf):
        op = self._mk("pool", fn, reads, writes, "d")
        sbuf.dma_sem = "pending"
        self.dma_bufs.append(sbuf)
        sbuf.dma_cnt = 1
        op.dsem = sbuf
        op.dval = 1
        op.cc = True
        return op

    def emit(self):
        nc = self.nc
        with ExitStack() as st:
            esem = {e: st.enter_context(nc.semaphore("sem_" + e)) for e in ENGS}
            for i, b in enumerate(self.dma_bufs):
                b.dma_sem = st.enter_context(nc.semaphore("dsem%d" % i))
            for e in ENGS:
                c = 0
                for op in self.ops[e]:
                    if op.kind == "c" and op.signal:
                        c += 1
                        op.semval = c
            block = st.enter_context(nc.Block())

            def run(e, h):
                for op in self.ops[e]:
                    for d in op.waits:
                        if d.kind == "d":
                            h.wait_ge(d.dsem.dma_sem, d.dval)
                        else:
                            h.wait_ge(esem[d.eng], d.semval)
                    ins = op.fn(h)
                    if op.cc:
                        ins.then_inc(op.dsem.dma_sem)
                    elif op.kind == "d":
                        ins.then_inc(op.dsem.dma_sem, 16)
                    elif op.signal:
                        ins.then_inc(esem[e], 1)
                if e == "sp":
                    for b in self.dma_bufs:
                        h.wait_ge(b.dma_sem, b.dma_cnt)

            @block.tensor
            def _(h):
                run("pe", h)

            @block.scalar
            def _(h):
                run("act", h)

            @block.vector
            def _(h):
                run("dve", h)

            @block.gpsimd
            def _(h):
                run("pool", h)

            @block.sync
            def _(h):
                run("sp", h)


class Slab:
    __slots__ = ("t", "b")

    def __init__(self, t, b):
        self.t = t
        self.b = b


class _Stop(Exception):
    pass


def build_program(n_mix, n_tok, n_cores=8, debug=None, stop_after=None):
    NT = n_mix
    n_pre, n_full = 0, n_mix
    nc = bass.Bass("TRN2", target_bir_lowering=False)

    def din(name, shape):
        return nc.dram_tensor(name, shape, F32, kind="ExternalInput").ap()

    xT_d = din("xT", [D, NT * T])
    xtok_d = din("xtok", [D, n_tok * T])
    sel_d = din("sel", [128, 2])
    memT_d = din("memT", [D, 256])
    consts_d = din("consts", [128, NCONST])
    params_d = din("params", [128, NPAR])
    w_in_d = din("w_in", [39, 128, 1024])
    lw_d = din("lw", [128, 512])
    g2a_d = din("g2a", [128, 512])
    g2b_d = din("g2b", [32, 512])
    yloc_t = [nc.dram_tensor("ymix_loc%d" % i, [512, T], F32) for i in range(NT)]
    yall_t = [nc.dram_tensor("ymix_all%d" % i, [1024, T], F32) for i in range(NT)]
    w_out_d = din("w_out", [8, 128, 1024])
    wq_d = din("wq", [8, 128, 1024])
    wk_d = din("wk", [8, 128, 1024])
    wv_d = din("wv", [8, 128, 1024])
    wo_d = din("wo", [8, 128, 1024])
    w1_d = din("w1", [22, 128, 1024])
    w3_d = din("w3", [22, 128, 1024])
    w2_d = din("w2", [24, 128, 1024])
    outT_d = nc.dram_tensor("outT", [D, n_tok * T], F32, kind="ExternalOutput").ap()
    dbg_out = {}

    s = Sched(nc)
    with ExitStack() as st:
        def sbt(name, shape):
            return st.enter_context(nc.sbuf_tensor(name, shape, F32))

        def pst(name):
            return st.enter_context(nc.psum_tensor(name, [128, 512], F32))

        CON = sbt("CON", [128, NCONST]); bCON = Buf("CON")
        PAR = sbt("PAR", [128, NPAR]); bPAR = Buf("PAR")
        PD = sbt("PD", [128, 16]); bPD = Buf("PD")
        SEL = sbt("SEL", [128, 2]); bSEL = Buf("SEL")
        LW = sbt("LW", [128, 512]); bLW = Buf("LW")
        G2A = sbt("G2A", [128, 512]); bG2A = Buf("G2A")
        G2B = sbt("G2B", [32, 512]); bG2B = Buf("G2B")
        bYL = [Buf("YL%d" % i) for i in range(NT)]
        bYALL = [Buf("YALL%d" % i) for i in range(NT)]; bCC = [Buf("CC%d" % i) for i in range(NT)]
        KTM = sbt("KTM", [128, 8 * 256]); bKTM = Buf("KTM")
        VM = sbt("VM", [128, 2 * 1024]); bVM = Buf("VM")
        SRW = sbt("SRW", [128, 4 * 64]); bSRW = [Buf("SRW%d" % i) for i in range(4)]
        SHG = sbt("SHG", [128, 4 * 128]); bSHG = [Buf("SHG%d" % i) for i in range(4)]
        CAR = sbt("CAR", [128, 15]); bCAR = [Buf("CAR%d" % i) for i in range(15)]
        H = sbt("H", [128, 8 * T]); bH = [Buf("H%d" % i) for i in range(8)]
        U = sbt("U", [128, 8 * T]); bU = [Buf("U%d" % i) for i in range(8)]
        YM = sbt("YM", [128, 8 * T]); bYM = [Buf("YM%d" % i) for i in range(8)]
        AR = sbt("AR", [128, 1024]); bAR = Buf("AR")
        AXT = sbt("AXT", [128, 1024]); bAXT = [Buf("AXT0"), Buf("AXT1")]
        EXT = [sbt("EXT%d" % i, [128, 516]) for i in range(2)]
        bEXT0 = [Buf("EXT0_%d" % i) for i in range(2)]
        bEXT = [Buf("EXT_%d" % i) for i in range(2)]
        NW = 8
        WR = [sbt("WR%d" % i, [128, 1024]) for i in range(NW)]
        bWR = [Buf("WR%d" % i) for i in range(NW)]
        NSLAB = 36
        slabs = [Slab(sbt("SL%d" % i, [128, T]), Buf("SL%d" % i)) for i in range(NSLAB)]
        free_slabs = deque(slabs)
        PS = [pst("PS%d" % i) for i in range(8)]
        bPS = [Buf("PS%d" % i) for i in range(8)]

        def Hc(k, n=T):
            return H[:, k * T:k * T + n]

        def Uc(k, n=T):
            return U[:, k * T:k * T + n]

        def YMc(k):
            return YM[:, k * T:(k + 1) * T]

        state = {"ps": 0, "wr": 0, "ext": 0, "wq": 0}

        def alloc():
            x = free_slabs.popleft()
            state["minfree"] = min(state.get("minfree", 99), len(free_slabs))
            return x

        def free(*sl):
            for x in sl:
                free_slabs.append(x)

        def psum():
            i = state["ps"]
            state["ps"] = (i + 1) % 5
            return PS[i], bPS[i]

        ident = CON[:, C_ID:C_ID + 128]
        ones = CON[:, C_ONES:C_ONES + 128]
        bones = CON[:, C_BONES:C_BONES + 128]
        idp = CON[:, C_IDP:C_IDP + 64]
        hgm = CON[:, C_HGM:C_HGM + 128]
        rwm2 = CON[:, C_RWM2:C_RWM2 + 128]
        rwm = CON[:, C_RWM:C_RWM + 512]
        scm = CON[:, C_SCM:C_SCM + 512]

        def P(name, j=0, m=128):
            c = PCOL[name] + j
            return PAR[0:m, c:c + 1]

        def PDc(j):
            return PD[:, j:j + 1]

        def mm(out, lhsT, rhs, start, stop, r, w):
            s.add("pe", lambda h: h.matmul(out, lhsT, rhs, start=start, stop=stop), reads=r, writes=w)

        def trp(out, in_, r, w):
            s.add("pe", lambda h: h.transpose(out, in_, ident), reads=list(r) + [bCON], writes=w)

        def act(out, in_, func, r, w, bias=None, scale=None):
            kw = {}
            if bias is not None:
                kw["bias"] = bias
            if scale is not None:
                kw["scale"] = scale
            s.add("act", lambda h: h.activation(out=out, in_=in_, func=func, **kw), reads=r, writes=w)

        def acopy(out, in_, r, w):
            s.add("act", lambda h: h.copy(out, in_), reads=r, writes=w)

        def vcopy(out, in_, r, w):
            s.add("dve", lambda h: h.tensor_copy(out=out, in_=in_), reads=r, writes=w)

        def pcopy(out, in_, r, w):
            s.add("pool", lambda h: h.tensor_copy(out=out, in_=in_), reads=r, writes=w)

        def tt(out, in0, in1, op, r, w):
            s.add("dve", lambda h: h.tensor_tensor(out=out, in0=in0, in1=in1, op=op), reads=r, writes=w)

        def ts(out, in0, s1, s2, op0, op1, r, w):
            if s2 is None:
                s.add("dve", lambda h: h.tensor_scalar(out=out, in0=in0, scalar1=s1, scalar2=None, op0=op0), reads=r, writes=w)
            else:
                s.add("dve", lambda h: h.tensor_scalar(out=out, in0=in0, scalar1=s1, scalar2=s2, op0=op0, op1=op1), reads=r, writes=w)

        def stt(out, in0, scalar, in1, op0, op1, r, w):
            s.add("dve", lambda h: h.scalar_tensor_tensor(out=out, in0=in0, scalar=scalar, in1=in1, op0=op0, op1=op1), reads=r, writes=w)

        def recip(out, in_, r, w):
            s.add("dve", lambda h: h.reciprocal(out=out, in_=in_), reads=r, writes=w)

        def scan(out, d0, d1, r, w):
            s.add("dve", lambda h: h.tensor_tensor_scan(out=out, data0=d0, data1=d1, initial=0.0, op0=ALU.mult, op1=ALU.add), reads=r, writes=w)

        def dma_in(eng, out, in_, wbufs):
            s.dma(eng, lambda h: h.dma_start(out=out, in_=in_), writes=wbufs, sbuf=wbufs[0])

        def wload(src):
            i = state["wr"]
            state["wr"] = (i + 1) % NW
            q = "sp" if (state["wq"] % 2 == 0) else "pool"
            state["wq"] += 1
            dma_in(q, WR[i][:, :], src, [bWR[i]])
            return WR[i], bWR[i]

        def v3(ap, j):
            return ap.rearrange("p (s j) -> p s j", j=j)

        def dbg(name, ap, shape, rbufs):
            if debug is None or name not in debug:
                return
            o = nc.dram_tensor("dbg_" + name, list(shape), F32, kind="ExternalOutput").ap()
            dbg_out[name] = o
            s.dma("pool", lambda h: h.dma_start(out=o, in_=ap), reads=rbufs, sbuf=rbufs[0])

        def rmsnorm(src, bsrc, gname, dst, bdst, n=T):
            ps, pb = psum()
            sq = [alloc(), alloc()]
            for k in range(8):
                q = sq[k % 2]
                act(q.t[:, 0:n], src[k], AF.Square, [bsrc[k]], [q.b])
                mm(ps[:, 0:n], ones, q.t[:, 0:n], k == 0, k == 7, [bCON, q.b], [pb])
            rs = alloc()
            act(rs.t[:, 0:n], ps[:, 0:n], AF.Ln, [pb], [rs.b], bias=NORM_EPS, scale=1.0 / D)
            act(rs.t[:, 0:n], rs.t[:, 0:n], AF.Exp, [rs.b], [rs.b], scale=-0.5)
            for k in range(8):
                stt(dst[k], src[k], P(gname, k), rs.t[:, 0:n], ALU.mult, ALU.mult, [bsrc[k], bPAR, rs.b], [bdst[k]])
            free(sq[0], sq[1], rs)

        def proj(wsrc, rhs_list, brhs, m=128, n=T, nk=8):
            W, wb = wload(wsrc)
            ps, pb = psum()
            for k in range(nk):
                mm(ps[0:m, 0:n], W[:, k * 128:k * 128 + m], rhs_list[k], k == 0, k == nk - 1, [wb, brhs[k]], [pb])
            return ps, pb

        def proj_in(c):
            m = IN_M[c]
            ps, pb = proj(w_in_d[c], [Uc(k) for k in range(8)], bU, m=m)
            return ps, pb, m

        def lerp_proj(c):
            ps, pb, m = proj_in(c)
            return lerp_tail(c, ps, pb, m)

        def lerp_proj_gen(c, ps, pb):
            W, wb = wload(w_in_d[c])
            for k in range(8):
                mm(ps[:, :], W[:, k * 128:(k + 1) * 128], Uc(k), k == 0, k == 7, [wb, bU[k]], [pb])
                yield None
            yield lerp_tail(c, ps, pb, 128)

        def lerp_tail(c, ps, pb, m):
            i = state["ext"]
            state["ext"] ^= 1
            e = EXT[i]
            acopy(e[0:m, 0:1], CAR[0:m, c:c + 1], [bCAR[c]], [bEXT0[i]])
            acopy(e[0:m, 1:513], ps[0:m, :], [pb], [bEXT[i]])
            acopy(CAR[0:m, c:c + 1], e[0:m, 512:513], [bEXT[i]], [bCAR[c]])
            d = alloc()
            tt(d.t[0:m, :], e[0:m, 0:512], e[0:m, 1:513], ALU.subtract, [bEXT0[i], bEXT[i]], [d.b])
            dst = alloc()
            stt(dst.t[0:m, :], d.t[0:m, :], P("mu", c, m), e[0:m, 1:513], ALU.mult, ALU.add, [d.b, bPAR, bEXT[i]], [dst.b])
            free(d)
            return dst

        def transpose4(src_ap_fn, rb, dst_fn, wb, evac):
            ps, pb = psum()
            for stq in range(4):
                trp(ps[:, stq * 128:(stq + 1) * 128], src_ap_fn(stq), rb, [pb])
            if evac == "act":
                acopy(dst_fn(), v3(ps[:, :], 128), [pb], wb)
            else:
                vcopy(dst_fn(), v3(ps[:, :], 128), [pb], wb)

        def chk(label):
            if stop_after is not None and label == stop_after:
                raise _Stop()

        def body():
            dma_in("sp", CON[:, :], consts_d, [bCON])
            dma_in("pool", PAR[:, :], params_d, [bPAR])
            dma_in("sp", LW[:, :], lw_d, [bLW])
            dma_in("pool", G2A[:, :], g2a_d, [bG2A])
            dma_in("sp", G2B[:, :], g2b_d, [bG2B])
            s.add("pool", lambda h: h.memset(SRW[:, :], 0.0), writes=bSRW)
            s.add("pool", lambda h: h.memset(SHG[:, :], 0.0), writes=bSHG)
            s.add("pool", lambda h: h.memset(CAR[:, :], 0.0), writes=bCAR)
            dma_in("pool", SEL[:, :], sel_d, [bSEL])
            ts(PD[:, 0:4], PAR[:, PCOL["k_a"]:PCOL["k_a"] + 4], -1.0, 1.0, ALU.mult, ALU.add, [bPAR], [bPD])
            tt(PD[:, 12:16], PAR[:, PCOL["l0"]:PCOL["l0"] + 4], PAR[:, PCOL["l1"]:PCOL["l1"] + 4], ALU.subtract, [bPAR], [bPD])
            act(PD[:, 4:8], PD[:, 12:16], AF.Sigmoid, [bPD], [bPD])
            ts(PD[:, 8:12], PD[:, 4:8], -1.0, 1.0, ALU.mult, ALU.add, [bPD], [bPD])

            if n_tok > 0:
                dma_in("pool", v3(H[:, 0:8 * 256], 256), memT_d.rearrange("(k p) t -> p k t", p=128), bH)
                msrc = [H[:, k * 256:(k + 1) * 256] for k in range(8)]
                mdst = [U[:, k * 256:(k + 1) * 256] for k in range(8)]
                rmsnorm(msrc, bH, "g_mem", mdst, bU, n=256)
                for c in range(8):
                    ps, pb = proj(wk_d[c], mdst, bU, n=256)
                    acopy(KTM[:, c * 256:(c + 1) * 256], ps[:, 0:256], [pb], [bKTM])
                vps = [psum() for _ in range(4)]
                for k in range(8):
                    W, wb = wload(wv_d[k])
                    for mc in range(2):
                        for nn in range(2):
                            ps, pb = vps[mc * 2 + nn]
                            mm(ps[:, :], mdst[k][:, mc * 128:(mc + 1) * 128], W[:, nn * 512:(nn + 1) * 512], k == 0, k == 7, [bU[k], wb], [pb])
                for mc in range(2):
                    for nn in range(2):
                        ps, pb = vps[mc * 2 + nn]
                        acopy(VM[:, mc * 1024 + nn * 512: mc * 1024 + (nn + 1) * 512], ps[:, :], [pb], [bVM])

            chk('prologue')
            for ti in range(NT):
                full = ti >= n_pre
                lastpre = (ti == n_pre - 1)
                t0 = ti * T
                dma_in("pool", v3(H[:, :], T), xT_d[:, t0:t0 + T].rearrange("(k p) t -> p k t", p=128), bH)
                rmsnorm([Hc(k) for k in range(8)], bH, "g_mix", [Uc(k) for k in range(8)], bU)
                chk('norm1')

                LO1 = lerp_proj(12)
                act(LO1.t[0:64, :], LO1.t[0:64, :], AF.Tanh, [LO1.b], [LO1.b])
                chk('lo1')
                if full:
                    LG1 = lerp_proj(13)
                    act(LG1.t[:, :], LG1.t[:, :], AF.Sigmoid, [LG1.b], [LG1.b])
                    LG2 = lerp_proj(14)
                    act(LG2.t[0:32, :], LG2.t[0:32, :], AF.Sigmoid, [LG2.b], [LG2.b])
                elif lastpre:
                    free(lerp_proj(13))
                    free(lerp_proj(14))

                def fetch_hg(hh):
                    d = {}
                    if full:
                        ps, pb, _ = proj_in(15 + hh)
                        d["Q"] = alloc()
                        act(d["Q"].t[:, :], ps[:, :], AF.Silu, [pb], [d["Q"].b])
                    ps, pb, _ = proj_in(19 + hh)
                    d["Fg"] = alloc()
                    act(d["Fg"].t[:, :], ps[:, :], AF.Sigmoid, [pb], [d["Fg"].b])
                    ps, pb, _ = proj_in(23 + hh)
                    d["Ii"] = alloc()
                    acopy(d["Ii"].t[:, :], ps[:, :], [pb], [d["Ii"].b])
                    if full:
                        ps, pb, _ = proj_in(27 + hh)
                        d["GS"] = alloc()
                        act(d["GS"].t[:, :], ps[:, :], AF.Silu, [pb], [d["GS"].b])
                    return d

                nxt = {"R": lerp_proj(0) if full else None, "K": lerp_proj(4), "V": lerp_proj(8)}
                if lastpre and not full:
                    free(lerp_proj(0))
                hg_next = None
                for hp in range(4):
                    ch = slice(hp * 128, (hp + 1) * 128)
                    R, K, V = nxt["R"], nxt["K"], nxt["V"]
                    vgen = None
                    if hp < 3:
                        nxt = {"R": lerp_proj(hp + 1) if full else None, "K": lerp_proj(4 + hp + 1), "V": None}
                        if lastpre and not full:
                            free(lerp_proj(hp + 1))
                        vgen = lerp_proj_gen(8 + hp + 1, PS[5], bPS[5])
                    else:
                        hg_next = fetch_hg(0)
                    ps, pb = psum()
                    mm(ps[:, :], LW[0:64, ch], LO1.t[0:64, :], True, True, [bLW, LO1.b], [pb])
                    SW = alloc()
                    act(SW.t[:, :], ps[:, :], AF.Sigmoid, [pb, bPAR], [SW.b], bias=P("w0", hp))
                    ps, pb = psum()
                    mm(ps[:, :], LW[64:128, ch], LO1.t[64:128, :], True, True, [bLW, LO1.b], [pb])
                    Aa = alloc()
                    act(Aa.t[:, :], ps[:, :], AF.Sigmoid, [pb, bPAR], [Aa.b], bias=P("a0", hp))
                    if full:
                        ps, pb = psum()
                        mm(ps[:, :], G2A[:, ch], LG1.t[:, :], True, False, [bG2A, LG1.b], [pb])
                        mm(ps[:, :], G2B[0:32, ch], LG2.t[0:32, :], False, True, [bG2B, LG2.b], [pb])
                        Gt = alloc()
                        acopy(Gt.t[:, :], ps[:, :], [pb], [Gt.b])
                    KK = alloc()
                    ts(KK.t[:, :], K.t[:, :], P("k_k", hp), None, ALU.mult, None, [K.b, bPAR], [KK.b])
                    sq = alloc()
                    act(sq.t[:, :], KK.t[:, :], AF.Square, [KK.b], [sq.b])
                    ps_ss, pb_ss = psum()
                    mm(ps_ss[:, :], bones, sq.t[:, :], True, True, [bCON, sq.b], [pb_ss])
                    VT = alloc()
                    transpose4(lambda q: V.t[:, q * 128:(q + 1) * 128], [V.b], lambda: v3(VT.t[:, :], 128), [VT.b], "act")
                    KM = alloc()
                    ts(KM.t[:, :], Aa.t[:, :], P("k_a", hp), PDc(hp), ALU.mult, ALU.add, [Aa.b, bPAR, bPD], [KM.b])
                    tt(KM.t[:, :], KM.t[:, :], K.t[:, :], ALU.mult, [KM.b, K.b], [KM.b])
                    free(K)
                    CS = alloc()
                    scan(CS.t[:, :], scm, SW.t[:, :], [bCON, SW.b], [CS.b])
                    Wi = alloc()
                    act(Wi.t[:, :], CS.t[:, :], AF.Exp, [CS.b], [Wi.b], scale=-CDEC)
                    Wn = alloc()
                    act(Wn.t[:, :], CS.t[:, :], AF.Exp, [CS.b], [Wn.b], scale=CDEC)
                    tt(SW.t[:, :], CS.t[:, :], SW.t[:, :], ALU.subtract, [CS.b, SW.b], [SW.b])
                    act(SW.t[:, :], SW.t[:, :], AF.Exp, [SW.b], [SW.b], scale=-CDEC)
                    free(CS)
                    ts(sq.t[:, :], ps_ss[:, :], 1e-24, None, ALU.max, None, [pb_ss], [sq.b])
                    act(sq.t[:, :], sq.t[:, :], AF.Ln, [sq.b], [sq.b])
                    act(sq.t[:, :], sq.t[:, :], AF.Exp, [sq.b], [sq.b], scale=-0.5)
                    AR4 = AR[:, :].rearrange("p (s two j) -> p s two j", two=2, j=128)
                    if full:
                        sq2 = alloc()
                        stt(sq2.t[:, :], R.t[:, :], P("r_k", hp), KM.t[:, :], ALU.mult, ALU.mult, [R.b, bPAR, KM.b], [sq2.b])
                        ps_bn, pb_bn = psum()
                        mm(ps_bn[:, :], bones, sq2.t[:, :], True, True, [bCON, sq2.b], [pb_bn])
                        tt(AR4[:, :, 1, :], v3(R.t[:, :], 128), v3(Wi.t[:, :], 128), ALU.mult, [R.b, Wi.b], [bAR])
                        free(R)
                    WCb = v3(Wi.t[:, :], 64)[:, :, 63:64].to_broadcast([128, 8, 64])
                    KT = alloc()
                    tt(KT.t[:, :], KM.t[:, :], Wn.t[:, :], ALU.mult, [KM.b, Wn.b], [KT.b])
                    free(KM)
                    KH = alloc()
                    tt(v3(KH.t[:, :], 64), v3(KT.t[:, :], 64), WCb, ALU.mult, [KT.b, Wi.b], [KH.b])
                    KHT = alloc()
                    transpose4(lambda q: KH.t[:, q * 128:(q + 1) * 128], [KH.b], lambda: v3(KHT.t[:, :], 128), [KHT.b], "act")
                    if full:
                        BON = alloc()
                        tt(BON.t[:, :], ps_bn[:, :], V.t[:, :], ALU.mult, [pb_bn, V.b], [BON.b])
                        free(sq2)
                    free(V, KH)
                    tt(KK.t[:, :], KK.t[:, :], sq.t[:, :], ALU.mult, [KK.b, sq.b], [KK.b])
                    free(sq)
                    BV = alloc()
                    tt(BV.t[:, :], KK.t[:, :], Aa.t[:, :], ALU.mult, [KK.b, Aa.b], [BV.b])
                    free(Aa)
                    stt(AR4[:, :, 0, :], v3(KK.t[:, :], 128), -1.0, v3(SW.t[:, :], 128), ALU.mult, ALU.mult, [KK.b, SW.b], [bAR])
                    free(KK, SW)
                    BT = alloc()
                    tt(BT.t[:, :], BV.t[:, :], Wn.t[:, :], ALU.mult, [BV.b, Wn.b], [BT.b])
                    free(BV, Wn)
                    BH = alloc()
                    tt(v3(BH.t[:, :], 64), v3(BT.t[:, :], 64), WCb, ALU.mult, [BT.b, Wi.b], [BH.b])
                    BHT = alloc()
                    transpose4(lambda q: BH.t[:, q * 128:(q + 1) * 128], [BH.b], lambda: v3(BHT.t[:, :], 128), [BHT.b], "dve")
                    free(BH)
                    AX5 = AXT[:, :].rearrange("p (s h x) -> p s h x", h=2, x=128)
                    ps, pb = psum()
                    for q in range(4):
                        trp(ps[:, q * 128:(q + 1) * 128], AR4[:, q, 0, :], [bAR], [pb])
                    psv = ps[:, :].rearrange("p (s h k) -> p s h k", h=2, k=64)
                    vcopy(AX5[:, :, 0, 0:64], psv[:, :, 0, :], [pb], [bAXT[0]])
                    vcopy(AX5[:, :, 1, 0:64], psv[:, :, 1, :], [pb], [bAXT[1]])
                    chk('front')

                    MA = [[None] * 4 for _ in range(2)]
                    AVP = [None, None]
                    ncol = 256 if full else 128
                    psR, pbR = PS[5], bPS[5]
                    psGa, pbGa = PS[6], bPS[6]
                    psGb, pbGb = PS[7], bPS[7]
                    for hd in range(2):
                        pq = slice(hd * 64, hd * 64 + 64)
                        for q in range(4):
                            tc = slice(q * 128, (q + 1) * 128)
                            ps, pb = psum()
                            if full:
                                mm(ps[:, 0:256], BT.t[pq, tc], AR[pq, q * 256:(q + 1) * 256], True, True, [BT.b, bAR], [pb])
                                mm(ps[:, 256:512], KT.t[pq, tc], AR[pq, q * 256:(q + 1) * 256], True, True, [KT.b, bAR], [pb])
                                m_ = alloc()
                                tt(m_.t[:, :], ps[:, :], rwm, ALU.mult, [pb, bCON], [m_.b])
                            else:
                                mm(ps[:, 0:128], BT.t[pq, tc], AR[pq, q * 256:q * 256 + 128], True, True, [BT.b, bAR], [pb])
                                mm(ps[:, 256:384], KT.t[pq, tc], AR[pq, q * 256:q * 256 + 128], True, True, [KT.b, bAR], [pb])
                                m_ = alloc()
                                tt(m_.t[:, 0:128], ps[:, 0:128], rwm[:, 0:128], ALU.mult, [pb, bCON], [m_.b])
                                tt(m_.t[:, 256:384], ps[:, 256:384], rwm[:, 256:384], ALU.mult, [pb, bCON], [m_.b])
                            MA[hd][q] = m_
                        chk('amat')
                        ps, pb = psum()
                        for q in range(4):
                            mm(ps[:, q * 128:(q + 1) * 128], AR[pq, q * 256:q * 256 + 128], BT.t[pq, q * 128:(q + 1) * 128], True, True, [bAR, BT.b], [pb])
                        Pa = alloc()
                        tt(v3(Pa.t[:, :], 128), v3(ps[:, :], 128), rwm2.unsqueeze(1).to_broadcast([128, 4, 128]), ALU.mult, [pb, bCON], [Pa.b])
                        Xa = alloc()
                        for q in range(4):
                            tt(Xa.t[:, q * 128:(q + 1) * 128], MA[hd][q].t[:, 0:128], ident, ALU.add, [MA[hd][q].b, bCON], [Xa.b])
                        Pcur = [Pa.t[:, q * 128:(q + 1) * 128] for q in range(4)]
                        PTcur = [MA[hd][q].t[:, 0:128] for q in range(4)]
                        bP = [Pa.b] * 4
                        bPT = [MA[hd][q].b for q in range(4)]
                        held = [Pa]
                        for lvl in range(1, 6):
                            psL, pbL = psum()
                            for q in range(4):
                                mm(psL[:, q * 128:(q + 1) * 128], PTcur[q], Pcur[q], True, True, [bPT[q], bP[q]], [pbL])
                            Pn = alloc()
                            acopy(Pn.t[:, :], psL[:, :], [pbL], [Pn.b])
                            if lvl < 5:
                                psT, pbT = psum()
                                for q in range(4):
                                    mm(psT[:, q * 128:(q + 1) * 128], Pcur[q], PTcur[q], True, True, [bP[q], bPT[q]], [pbT])
                                PTn = alloc()
                                vcopy(PTn.t[:, :], psT[:, :], [pbT], [PTn.b])
                            psU, pbU = psum()
                            for q in range(4):
                                mm(psU[:, q * 128:(q + 1) * 128], Pn.t[:, q * 128:(q + 1) * 128], Xa.t[:, q * 128:(q + 1) * 128], True, True, [Pn.b, Xa.b], [pbU])
                            Xn = alloc()
                            tt(Xn.t[:, :], psU[:, :], Xa.t[:, :], ALU.add, [pbU, Xa.b], [Xn.b])
                            free(Xa)
                            Xa = Xn
                            free(*held)
                            held = [Pn]
                            Pcur = [Pn.t[:, q * 128:(q + 1) * 128] for q in range(4)]
                            bP = [Pn.b] * 4
                            if lvl < 5:
                                held.append(PTn)
                                PTcur = [PTn.t[:, q * 128:(q + 1) * 128] for q in range(4)]
                                bPT = [PTn.b] * 4
                        free(*held)
                        chk('chain')
                        ps, pb = psum()
                        VT3 = v3(VT.t[:, :], 128)
                        for q in range(4):
                            mm(ps[:, q * 64:(q + 1) * 64], MA[hd][q].t[:, 256:384], VT3[:, q, pq], True, True, [MA[hd][q].b, VT.b], [pb])
                        acopy(AX5[:, :, hd, 64:128], v3(ps[:, 0:256], 64), [pb], [bAXT[hd]])
                        ps, pb = psum()
                        for q in range(4):
                            mm(ps[:, q * 128:(q + 1) * 128], Xa.t[:, q * 128:(q + 1) * 128], AX5[:, q, hd, :], True, True, [Xa.b, bAXT[hd]], [pb])
                        av = alloc()
                        acopy(av.t[:, :], ps[:, :], [pb], [av.b])
                        AVP[hd] = av
                        free(Xa)
                        av3 = v3(av.t[:, :], 128)
                        chk('xav')
                        if full:
                            for q in range(4):
                                tc = slice(q * 128, (q + 1) * 128)
                                mm(psR[pq, tc], av3[:, q, 0:64], MA[hd][q].t[:, 128:256], True, False, [av.b, MA[hd][q].b], [pbR])
                                mm(psR[pq, tc], CON[pq, C_ID + hd * 64:C_ID + hd * 64 + 64], AR[pq, q * 256 + 128:(q + 1) * 256], False, True, [bCON, bAR], [pbR])
                        chk('rp')
                        BHT3 = v3(BHT.t[:, :], 128)
                        KHT3 = v3(KHT.t[:, :], 128)
                        for c in range(8):
                            q = c // 2
                            tb = slice((c % 2) * 64, (c % 2) * 64 + 64)
                            bank, bb = (psGa, pbGa) if c % 2 == 0 else (psGb, pbGb)
                            c0 = (c // 2) * 128
                            mm(bank[pq, c0:c0 + 64], av3[tb, q, 0:64], BHT3[tb, q, pq], True, True, [av.b, BHT.b], [bb])
                            mm(bank[pq, c0 + 64:c0 + 128], BHT3[tb, q, pq], av3[tb, q, 64:128], True, False, [av.b, BHT.b], [bb])
                            mm(bank[pq, c0 + 64:c0 + 128], KHT3[tb, q, pq], VT3[tb, q, pq], False, True, [KHT.b, VT.b], [bb])
                    free(BT, KT, BHT, KHT)
                    chk('gh')
                    GH = [alloc(), alloc()]
                    acopy(GH[0].t[:, :], psGa[:, :], [pbGa], [GH[0].b])
                    acopy(GH[1].t[:, :], psGb[:, :], [pbGb], [GH[1].b])
                    for c in range(8):
                        g = GH[c % 2]
                        gv = g.t[:, (c // 2) * 128:(c // 2) * 128 + 64]
                        stt(gv, idp, Wi.t[:, c * 64 + 63:c * 64 + 64], gv, ALU.mult, ALU.add, [bCON, Wi.b, g.b], [g.b])
                    free(Wi)
                    chk('ghe')
                    if full:
                        RP = alloc()
                        acopy(RP.t[:, :], psR[:, :], [pbR], [RP.b])
                    STT = alloc()
                    pcopy(STT.t[:, 0:64], SRW[:, hp * 64:(hp + 1) * 64], [bSRW[hp]], [STT.b])
                    chk('stcopy')
                    for c in range(8):
                        g = GH[c % 2]
                        c0 = (c // 2) * 128
                        for hd in range(2):
                            pq = slice(hd * 64, hd * 64 + 64)
                            ps, pb = psum()
                            mm(ps[pq, 0:64], g.t[pq, c0:c0 + 64], STT.t[pq, c * 64:(c + 1) * 64], True, True, [g.b, STT.b], [pb])
                            if c < 7:
                                tt(STT.t[pq, (c + 1) * 64:(c + 2) * 64], ps[pq, 0:64], g.t[pq, c0 + 64:c0 + 128], ALU.add, [pb, g.b], [STT.b])
                            else:
                                tt(SRW[pq, hp * 64:(hp + 1) * 64], ps[pq, 0:64], g.t[pq, c0 + 64:c0 + 128], ALU.add, [pb, g.b], [bSRW[hp]])
                        if vgen is not None:
                            next(vgen)
                    if vgen is not None:
                        nxt["V"] = next(vgen)
                    free(GH[0], GH[1])
                    chk('state')
                    if full:
                        psY, pbY = psum()
                        for hd in range(2):
                            pq = slice(hd * 64, hd * 64 + 64)
                            av3 = v3(AVP[hd].t[:, :], 128)
                            for q in range(4):
                                c0, c1 = 2 * q, 2 * q + 1
                                mm(psY[pq, c0 * 64:c0 * 64 + 64], STT.t[pq, c0 * 64:c0 * 64 + 64], RP.t[pq, c0 * 64:c0 * 64 + 64], True, False, [STT.b, RP.b], [pbY])
                                mm(psY[pq, c1 * 64:c1 * 64 + 64], STT.t[pq, c1 * 64:c1 * 64 + 64], RP.t[pq, c1 * 64:c1 * 64 + 64], False, False, [STT.b, RP.b], [pbY])
                                mm(psY[pq, q * 128:(q + 1) * 128], av3[:, q, 64:128], MA[hd][q].t[:, 128:256], False, False, [AVP[hd].b, MA[hd][q].b], [pbY])
                                mm(psY[pq, q * 128:(q + 1) * 128], VT3[:, q, pq], MA[hd][q].t[:, 384:512], False, True, [VT.b, MA[hd][q].b], [pbY])
                        free(RP)
                    free(STT, VT, AVP[0], AVP[1])
                    for hd in range(2):
                        free(*MA[hd])
                    if full:
                        Y = alloc()
                        acopy(Y.t[:, :], psY[:, :], [pbY], [Y.b])
                        if hp == 0:
                            dbg("y0", Y.t[:, :], [128, 512], [Y.b])
                        ps, pb = psum()
                        mm(ps[:, :], bones, Y.t[:, :], True, True, [bCON, Y.b], [pb])
                        psg, pbg, _ = proj_in(31 + hp)
                        GA = alloc()
                        act(GA.t[:, :], psg[:, :], AF.Sigmoid, [pbg], [GA.b])
                        stt(Y.t[:, :], ps[:, :], -1.0 / 64, Y.t[:, :], ALU.mult, ALU.add, [pb, Y.b], [Y.b])
                        sq = alloc()
                        act(sq.t[:, :], Y.t[:, :], AF.Square, [Y.b], [sq.b])
                        ps, pb = psum()
                        mm(ps[:, :], bones, sq.t[:, :], True, True, [bCON, sq.b], [pb])
                        act(sq.t[:, :], ps[:, :], AF.Ln, [pb], [sq.b], bias=RW_LN_EPS, scale=1.0 / 64)
                        act(sq.t[:, :], sq.t[:, :], AF.Exp, [sq.b], [sq.b], scale=-0.5)
                        tt(Y.t[:, :], Y.t[:, :], sq.t[:, :], ALU.mult, [Y.b, sq.b], [Y.b])
                        ts(Y.t[:, :], Y.t[:, :], P("ln_w", hp), P("ln_b", hp), ALU.mult, ALU.add, [Y.b, bPAR], [Y.b])
                        tt(Y.t[:, :], Y.t[:, :], BON.t[:, :], ALU.add, [Y.b, BON.b], [Y.b])
                        tt(Y.t[:, :], Y.t[:, :], Gt.t[:, :], ALU.mult, [Y.b, Gt.b], [Y.b])
                        tt(YMc(hp), Y.t[:, :], GA.t[:, :], ALU.mult, [Y.b, GA.b], [bYM[hp]])
                        free(GA)
                        if hp == 0:
                            dbg("ya0", YMc(0), [128, 512], [bYM[0]])
                        free(Y, sq, BON, Gt)
                    chk('pair0')
                free(LO1)
                if full:
                    free(LG1, LG2)
                chk('rwkv')

                for hh in range(4):
                    cur = hg_next
                    hg_next = fetch_hg(hh + 1) if hh < 3 else None
                    Q = cur.get("Q")
                    Fg = cur["Fg"]
                    ts(Fg.t[:, :], Fg.t[:, :], PDc(8 + hh), PDc(4 + hh), ALU.mult, ALU.add, [Fg.b, bPD], [Fg.b])
                    LF = alloc()
                    act(LF.t[:, :], Fg.t[:, :], AF.Ln, [Fg.b], [LF.b])
                    ts(Fg.t[:, :], Fg.t[:, :], -1.0, 1.0, ALU.mult, ALU.add, [Fg.b], [Fg.b])
                    Ii = cur["Ii"]
                    IT = alloc()
                    transpose4(lambda q: Ii.t[:, q * 128:(q + 1) * 128], [Ii.b], lambda: v3(IT.t[:, :], 128), [IT.b], "dve")
                    free(Ii)
                    GS = cur.get("GS")
                    Bc = alloc()
                    scan(Bc.t[:, :], scm, LF.t[:, :], [bCON, LF.b], [Bc.b])
                    free(LF)
                    B3 = v3(Bc.t[:, :], 64)
                    E3 = alloc()
                    act(E3.t[:, :], Bc.t[:, :], AF.Exp, [Bc.b], [E3.b])
                    if full:
                        BM = alloc()
                        tt(v3(BM.t[:, :], 64), B3, B3[:, :, 31:32].to_broadcast([128, 8, 64]), ALU.subtract, [Bc.b], [BM.b])
                        E1 = alloc()
                        act(E1.t[:, :], BM.t[:, :], AF.Exp, [BM.b], [E1.b])
                        act(BM.t[:, :], BM.t[:, :], AF.Exp, [BM.b], [BM.b], scale=-1.0)
                        tt(E1.t[:, :], E1.t[:, :], Q.t[:, :], ALU.mult, [E1.b, Q.b], [E1.b])
                        tt(BM.t[:, :], BM.t[:, :], Fg.t[:, :], ALU.mult, [BM.b, Fg.b], [BM.b])
                        QT, KTh = E1, BM
                        tt(Q.t[:, :], Q.t[:, :], E3.t[:, :], ALU.mult, [Q.b, E3.b], [Q.b])
                    BE = alloc()
                    tt(v3(BE.t[:, :], 64), B3, B3[:, :, 63:64].to_broadcast([128, 8, 64]), ALU.subtract, [Bc.b], [BE.b])
                    free(Bc)
                    act(BE.t[:, :], BE.t[:, :], AF.Exp, [BE.b], [BE.b], scale=-1.0)
                    tt(BE.t[:, :], BE.t[:, :], Fg.t[:, :], ALU.mult, [BE.b, Fg.b], [BE.b])
                    free(Fg)
                    KHT = alloc()
                    transpose4(lambda q: BE.t[:, q * 128:(q + 1) * 128], [BE.b], lambda: v3(KHT.t[:, :], 128), [KHT.b], "act")
                    free(BE)
                    IT3 = v3(IT.t[:, :], 128)
                    KHT3 = v3(KHT.t[:, :], 128)
                    HH = [alloc(), alloc()]
                    for bi in range(2):
                        ps, pb = psum()
                        tb = slice(bi * 64, bi * 64 + 64)
                        for q in range(4):
                            mm(ps[:, q * 128:(q + 1) * 128], KHT3[tb, q, :], IT3[tb, q, :], True, True, [KHT.b, IT.b], [pb])
                        acopy(HH[bi].t[:, :], ps[:, :], [pb], [HH[bi].b])
                    free(KHT)
                    SS = [alloc(), alloc()]

                    def SSc(c):
                        return SS[c // 4].t[:, (c % 4) * 128:(c % 4 + 1) * 128]

                    pcopy(SSc(0), SHG[:, hh * 128:(hh + 1) * 128], [bSHG[hh]], [SS[0].b])
                    for c in range(8):
                        dE = E3.t[:, c * 64 + 63:c * 64 + 64]
                        hsrc = HH[c % 2].t[:, (c // 2) * 128:(c // 2 + 1) * 128]
                        if c < 7:
                            stt(SSc(c + 1), SSc(c), dE, hsrc, ALU.mult, ALU.add, [SS[c // 4].b, E3.b, HH[c % 2].b], [SS[(c + 1) // 4].b])
                        else:
                            stt(SHG[:, hh * 128:(hh + 1) * 128], SSc(c), dE, hsrc, ALU.mult, ALU.add, [SS[1].b, E3.b, HH[c % 2].b], [bSHG[hh]])
                    free(E3, HH[0], HH[1])
                    if full:
                        ps, pb = psum()
                        for q in range(4):
                            tc = slice(q * 128, (q + 1) * 128)
                            mm(ps[:, tc], KTh.t[:, tc], QT.t[:, tc], True, True, [KTh.b, QT.b], [pb])
                        SM = alloc()
                        tt(v3(SM.t[:, :], 128), v3(ps[:, :], 128), hgm.unsqueeze(1).to_broadcast([128, 4, 128]), ALU.mult, [pb, bCON], [SM.b])
                        free(QT, KTh)
                        psO, pbO = psum()
                        for q in range(4):
                            c0, c1 = 2 * q, 2 * q + 1
                            mm(psO[:, c0 * 64:c0 * 64 + 64], SSc(c0), Q.t[:, c0 * 64:c0 * 64 + 64], True, False, [SS[c0 // 4].b, Q.b], [pbO])
                            mm(psO[:, c1 * 64:c1 * 64 + 64], SSc(c1), Q.t[:, c1 * 64:c1 * 64 + 64], False, False, [SS[c1 // 4].b, Q.b], [pbO])
                            mm(psO[:, q * 128:(q + 1) * 128], IT3[:, q, :], SM.t[:, q * 128:(q + 1) * 128], False, True, [IT.b, SM.b], [pbO])
                        free(Q)
                        sq = SM
                        act(sq.t[:, :], psO[:, :], AF.Square, [pbO], [sq.b])
                        psg, pbg, _ = proj_in(35 + hh)
                        GB = alloc()
                        act(GB.t[:, :], psg[:, :], AF.Sigmoid, [pbg], [GB.b])
                        ps, pb = psum()
                        mm(ps[:, :], ones, sq.t[:, :], True, True, [bCON, sq.b], [pb])
                        act(sq.t[:, :], ps[:, :], AF.Ln, [pb], [sq.b], bias=NORM_EPS, scale=1.0 / 128)
                        act(sq.t[:, :], sq.t[:, :], AF.Exp, [sq.b], [sq.b], scale=-0.5)
                        O = alloc()
                        tt(O.t[:, :], psO[:, :], sq.t[:, :], ALU.mult, [pbO, sq.b], [O.b])
                        stt(O.t[:, :], O.t[:, :], P("hg_g"), GS.t[:, :], ALU.mult, ALU.mult, [O.b, bPAR, GS.b], [O.b])
                        tt(O.t[:, :], O.t[:, :], GB.t[:, :], ALU.mult, [O.b, GB.b], [O.b])
                        free(GB)
                        if hh == 0:
                            dbg("yb0", O.t[:, :], [128, 512], [O.b])
                        tt(YMc(hh), YMc(hh), O.t[:, :], ALU.add, [bYM[hh], O.b], [bYM[hh]])
                        free(O, sq, GS)
                    free(IT, SS[0], SS[1])
                    chk('hg0')

                s.dma("sp", lambda h, ti=ti: h.dma_start(out=yloc_t[ti].ap().rearrange("(k p) t -> p k t", p=128), in_=v3(YM[:, 0:4 * T], T)),
                      reads=bYM[0:4], writes=[bYL[ti]], sbuf=bYM[0])
                groups = [[2 * g, 2 * g + 1] for g in range(n_cores // 2)]
                s.coll(lambda h, ti=ti: h.collective_compute("AllGather", ALU.bypass, replica_groups=groups,
                                                             ins=[yloc_t[ti].ap().opt()], outs=[yall_t[ti].ap().opt()]),
                       reads=[bYL[ti]], writes=[bYALL[ti]], sbuf=bCC[ti])

            for ti in range(n_tok):
                t0 = ti * T
                dma_in("pool", v3(H[:, :], T), xtok_d[:, t0:t0 + T].rearrange("(k p) t -> p k t", p=128), bH)
                s.dma("pool", lambda h, ti=ti: h.dma_start(out=v3(U[:, :], T), in_=yall_t[ti].ap().rearrange("(k p) t -> p k t", p=128)),
                      reads=[bYALL[ti]], writes=bU, sbuf=bU[0])
                YB = [alloc() for _ in range(8)]
                for k in range(8):
                    s.dma("sp", lambda h, ti=ti, k=k, dst=YB[k].t: h.dma_start(out=dst[:, :], in_=yall_t[n_tok + ti].ap()[k * 128:(k + 1) * 128, :]),
                          reads=[bYALL[n_tok + ti]], writes=[YB[k].b], sbuf=YB[k].b)
                for k in range(8):
                    ts(YMc(k), Uc(k), SEL[:, 0:1], None, ALU.mult, None, [bU[k], bSEL], [bYM[k]])
                    stt(YMc(k), YB[k].t[:, :], SEL[:, 1:2], YMc(k), ALU.mult, ALU.add, [YB[k].b, bSEL, bYM[k]], [bYM[k]])
                free(*YB)

                for oc in range(8):
                    ps, pb = proj(w_out_d[oc], [YMc(k) for k in range(8)], bYM)
                    tt(Hc(oc), Hc(oc), ps[:, :], ALU.add, [bH[oc], pb], [bH[oc]])
                if ti == 0:
                    dbg("h1", H[:, 0:T], [128, T], [bH[0]])

                rmsnorm([Hc(k) for k in range(8)], bH, "g_xa", [Uc(k) for k in range(8)], bU)
                QX = []
                for c in range(8):
                    ps, pb = proj(wq_d[c], [Uc(k) for k in range(8)], bU)
                    qx = alloc()
                    acopy(qx.t[:, :], ps[:, :], [pb], [qx.b])
                    QX.append(qx)
                OX = []
                for a in range(4):
                    Es = []
                    for mc in range(2):
                        ps, pb = psum()
                        for j in range(2):
                            c = 2 * a + j
                            mm(ps[:, :], KTM[:, c * 256 + mc * 128:c * 256 + (mc + 1) * 128], QX[c].t[:, :], j == 0, j == 1, [bKTM, QX[c].b], [pb])
                        e = alloc()
                        act(e.t[:, :], ps[:, :], AF.Exp, [pb], [e.b], scale=1.0 / 16)
                        Es.append(e)
                    ps, pb = psum()
                    mm(ps[:, :], ones, Es[0].t[:, :], True, False, [bCON, Es[0].b], [pb])
                    mm(ps[:, :], ones, Es[1].t[:, :], False, True, [bCON, Es[1].b], [pb])
                    rd = alloc()
                    recip(rd.t[:, :], ps[:, :], [pb], [rd.b])
                    for j in range(2):
                        c = 2 * a + j
                        ps, pb = psum()
                        mm(ps[:, :], VM[:, c * 128:(c + 1) * 128], Es[0].t[:, :], True, False, [bVM, Es[0].b], [pb])
                        mm(ps[:, :], VM[:, 1024 + c * 128:1024 + (c + 1) * 128], Es[1].t[:, :], False, True, [bVM, Es[1].b], [pb])
                        ox = alloc()
                        tt(ox.t[:, :], ps[:, :], rd.t[:, :], ALU.mult, [pb, rd.b], [ox.b])
                        OX.append(ox)
                    free(Es[0], Es[1], rd)
                free(*QX)
                for oc in range(8):
                    ps, pb = proj(wo_d[oc], [OX[k].t[:, :] for k in range(8)], [OX[k].b for k in range(8)])
                    tt(Hc(oc), Hc(oc), ps[:, :], ALU.add, [bH[oc], pb], [bH[oc]])
                free(*OX)
                if ti == 0:
                    dbg("h2", H[:, 0:T], [128, T], [bH[0]])

                rmsnorm([Hc(k) for k in range(8)], bH, "g_ffn", [Uc(k) for k in range(8)], bU)
                HID = []
                for j in range(22):
                    ps1, pb1 = proj(w1_d[j], [Uc(k) for k in range(8)], bU)
                    ps3, pb3 = proj(w3_d[j], [Uc(k) for k in range(8)], bU)
                    hj = alloc()
                    act(hj.t[:, :], ps1[:, :], AF.Silu, [pb1], [hj.b])
                    tt(hj.t[:, :], hj.t[:, :], ps3[:, :], ALU.mult, [hj.b, pb3], [hj.b])
                    HID.append(hj)
                for oc in range(8):
                    ps, pb = psum()
                    for g in range(3):
                        W, wb = wload(w2_d[oc * 3 + g])
                        nk = 8 if g < 2 else 6
                        for kk in range(nk):
                            kidx = g * 8 + kk
                            mm(ps[:, :], W[:, kk * 128:(kk + 1) * 128], HID[kidx].t[:, :], kidx == 0, kidx == 21, [wb, HID[kidx].b], [pb])
                    tt(Hc(oc), Hc(oc), ps[:, :], ALU.add, [bH[oc], pb], [bH[oc]])
                free(*HID)

                rmsnorm([Hc(k) for k in range(8)], bH, "g_fin", [Uc(k) for k in range(8)], bU)
                o0 = ti * T
                s.dma("pool", lambda h, o0=o0: h.dma_start(out=outT_d[:, o0:o0 + T].rearrange("(k p) t -> p k t", p=128), in_=v3(U[:, :], T)),
                      reads=bU, sbuf=bU[0])

        try:
            body()
        except _Stop:
            pass
        s.emit()
    nc._minfree = state.get('minfree')
    nc._sched_stats = {e: (len(s.ops[e]), sum(1 for o in s.ops[e] if o.signal)) for e in ENGS}
    return nc, dbg_out


def _chunk_w(w, ncol_chunks=None, chunks=None):
    K, N = w.shape
    nk = K // 128
    if chunks is None:
        chunks = [(c * 128, 128) for c in range(N // 128)]
    out = np.zeros((len(chunks), 128, nk * 128), np.float32)
    w3 = w.reshape(nk, 128, N)
    for i, (c0, m) in enumerate(chunks):
        blk = w3[:, :, c0:c0 + m]
        o = out[i].reshape(128, nk, 128)
        o[:, :, :m] = blk.transpose(1, 0, 2)
    return out


def _vec_cols(v):
    v = np.asarray(v, np.float32).reshape(-1)
    n = v.shape[0] // 128
    return np.ascontiguousarray(v.reshape(n, 128).T)


def make_consts():
    c = np.zeros((128, NCONST), np.float32)
    idx = np.arange(128)
    c[:, C_ID:C_ID + 128] = np.eye(128)
    c[:, C_ONES:C_ONES + 128] = 1.0
    same = (idx[:, None] // 64) == (idx[None, :] // 64)
    c[:, C_BONES:C_BONES + 128] = same
    c[:, C_IDP:C_IDP + 64] = (idx[:, None] % 64) == np.arange(64)[None, :]
    le = idx[:, None] <= idx[None, :]
    lt = idx[:, None] < idx[None, :]
    c[:, C_HGM:C_HGM + 128] = same & le
    c[:, C_RWM2:C_RWM2 + 128] = same & (idx[:, None] > idx[None, :])
    c[:, C_RWM:C_RWM + 512] = np.concatenate([same & lt, same & le, same & lt, same & le], 1)
    sm = np.ones(512, np.float32)
    sm[::64] = 0.0
    c[:, C_SCM:C_SCM + 512] = sm[None, :]
    return c


def prep_shared(inp):
    f = lambda k: np.asarray(inp[k], np.float32)
    w2 = f("ffn_w2")[0]
    w2p = np.zeros((24 * 128, 1024), np.float32)
    w2p[:2816] = w2
    w2c = _chunk_w(w2p)
    w2g = np.ascontiguousarray(w2c.reshape(8, 128, 3, 1024).transpose(0, 2, 1, 3).reshape(24, 128, 1024))
    return {
        "consts": make_consts(),
        "w_out": _chunk_w(f("w_out")[0]),
        "wq": _chunk_w(f("xa_wq")[0]),
        "wk": _chunk_w(f("xa_wk")[0]),
        "wv": np.ascontiguousarray(f("xa_wv")[0].reshape(8, 128, 1024)),
        "wo": _chunk_w(f("xa_wo")[0]),
        "w1": _chunk_w(f("ffn_w1")[0]),
        "w3": _chunk_w(f("ffn_w3")[0]),
        "w2": w2g,
    }


def prep_half(inp, half):
    f = lambda k: np.asarray(inp[k], np.float32)
    par = np.zeros((128, NPAR), np.float32)
    own = slice(4 * half, 4 * half + 4)

    def put(name, v, sel=None):
        cols = _vec_cols(v)
        if sel is not None:
            cols = cols[:, sel]
        par[:, PCOL[name]:PCOL[name] + cols.shape[1]] = cols

    chunks = own_chunks(half)
    put("g_mix", f("norm_mix_g")[0])
    mu_full = f("rw_mu")[0]
    for c in range(15):
        c0, m = chunks[c]
        par[:m, PCOL["mu"] + c] = mu_full[c0:c0 + m]
    for name, key in (("w0", "rw_w0"), ("a0", "rw_a0"), ("k_k", "rw_k_k"), ("k_a", "rw_k_a"), ("ln_w", "rw_ln_w"), ("ln_b", "rw_ln_b")):
        put(name, f(key)[0], own)
    put("r_k", f("rw_r_k")[0].reshape(-1), own)
    put("l0", f("hg_lb_logits")[0], own)
    put("l1", f("hg_lb_logits")[1], own)
    par[:, PCOL["hg_g"]] = f("hg_norm_g")[0]
    put("g_xa", f("norm_xa_g")[0]); put("g_mem", f("norm_mem_g")[0]); put("g_ffn", f("norm_ffn_g")[0])
    put("g_fin", f("norm_final_g"))
    cs = slice(512 * half, 512 * half + 512)
    sel = np.zeros((128, 2), np.float32)
    sel[:, half] = 1.0
    return {
        "params": par,
        "w_in": _chunk_w(f("w_in")[0], chunks=chunks),
        "lw": np.ascontiguousarray(np.concatenate([f("rw_w2")[0], f("rw_a2")[0]], 0)[:, cs]),
        "g2a": np.ascontiguousarray(f("rw_g2")[0][:128, cs]),
        "g2b": np.ascontiguousarray(f("rw_g2")[0][128:160, cs]),
        "sel": sel,
    }


N_MIX = 16
N_TOK = 8


def kernel(**inputs):
    x = np.asarray(inputs["x"], np.float32)
    mem = np.asarray(inputs["mem"], np.float32)
    B, S, _ = x.shape
    sh = prep_shared(inputs)
    hv = [prep_half(inputs, 0), prep_half(inputs, 1)]
    half = S // 2
    in_maps = []
    for core in range(8):
        b, hf = core // 2, core % 2
        m = dict(sh)
        m.update(hv[hf])
        xt = np.ascontiguousarray(x[b].T)
        m["xT"] = xt
        m["xtok"] = np.ascontiguousarray(xt[:, hf * half:(hf + 1) * half])
        m["memT"] = np.ascontiguousarray(mem[b].T)
        in_maps.append(m)
    nc, _ = build_program(N_MIX, N_TOK, 8)
    res = run_bass_kernel_spmd(nc, in_maps, core_ids=list(range(8)))
    out = np.empty((B, S, D), np.float32)
    for core in range(8):
        b, hf = core // 2, core % 2
        out[b, hf * half:(hf + 1) * half] = res.results[core]["outT"].T
    return out
```
